# Optimizing a Trainium2 kernel written in Bass

```python
import math
import jax, jax.numpy as jnp
from jax import lax
import numpy as np

D_MODEL = 4096
BATCH = 2
SEQ = 8192
DEPTH = 1

MIX_WIDTH = D_MODEL
GROUP_WIDTH = MIX_WIDTH // 2
GLA_HEADS = 4
GLA_DV = GROUP_WIDTH // GLA_HEADS
GLA_DK = GLA_DV // 2
GLA_GATE_RANK = 16
GLA_TAU = 16.0
GLA_CHUNK = 64
MLSTM_HEADS = 4
MLSTM_DV = GROUP_WIDTH // MLSTM_HEADS
MLSTM_DK = MLSTM_DV // 2
MLSTM_CHUNK = 64
CONV_WIDTH = 4
MEM_TOKENS = 256
XATTN_HEADS = 4
XATTN_HEAD_DIM = D_MODEL // XATTN_HEADS
FFN_HIDDEN = 256 * math.ceil(8 * D_MODEL / (3 * 256))
EPS = 1e-6

SPLIT_SIZES = (
    GLA_HEADS * GLA_DK,
    GLA_HEADS * GLA_DK,
    GLA_HEADS * GLA_DV,
    GLA_HEADS * GLA_DV,
    GLA_GATE_RANK,
    MLSTM_HEADS * MLSTM_DK,
    MLSTM_HEADS * MLSTM_DK,
    MLSTM_HEADS * MLSTM_DV,
    MLSTM_HEADS * MLSTM_DV,
    MLSTM_HEADS,
    MLSTM_HEADS,
)
IN_COLS = sum(SPLIT_SIZES)

kernel_name = "hybrid_gla_mlstm_parallel_heads"


def rmsnorm(x, g):
    xf = x.astype(jnp.float32)
    y = xf * lax.rsqrt(jnp.mean(xf * xf, axis=-1, keepdims=True) + EPS)
    return (y * g.astype(jnp.float32)).astype(x.dtype)


def head_rmsnorm(h, g):
    y = h * lax.rsqrt(jnp.mean(h * h, axis=-1, keepdims=True) + EPS)
    return y * g.astype(jnp.float32)


def to_chunks(t, c):
    b, s, h = t.shape[:3]
    t = t.reshape((b, s // c, c, h) + t.shape[3:])
    return jnp.moveaxis(t, (1, 3), (0, 2))


def from_chunks(t):
    t = jnp.moveaxis(t, (0, 2), (1, 3))
    b, n, c, h, d = t.shape
    return t.reshape(b, n * c, h, d)


def gla_chunked(q, k, v, log_a):
    bsz = q.shape[0]
    qc, kc, vc = to_chunks(q, GLA_CHUNK), to_chunks(k, GLA_CHUNK), to_chunks(v, GLA_CHUNK)
    bc = lax.cumsum(to_chunks(log_a, GLA_CHUNK), axis=3)
    mask = jnp.tril(jnp.ones((GLA_CHUNK, GLA_CHUNK), dtype=bool))

    def body(state, inp):
        qb, kb, vb, bb = inp
        diff = jnp.where(mask[None, None, :, :, None],
                         bb[:, :, :, None, :] - bb[:, :, None, :, :], -jnp.inf)
        attn = jnp.sum(qb[:, :, :, None, :] * kb[:, :, None, :, :] * jnp.exp(diff), axis=-1)
        out = (jnp.einsum('bhts,bhsv->bhtv', attn, vb)
               + jnp.einsum('bhtd,bhdv->bhtv', qb * jnp.exp(bb), state))
        b_last = bb[:, :, -1:, :]
        state = (jnp.exp(b_last[:, :, 0, :, None]) * state
                 + jnp.einsum('bhsd,bhsv->bhdv', kb * jnp.exp(b_last - bb), vb))
        return state, out

    state0 = jnp.zeros((bsz, GLA_HEADS, GLA_DK, GLA_DV), jnp.float32)
    _, out = lax.scan(body, state0, (qc, kc, vc, bc))
    return from_chunks(out)


def mlstm_chunked(q, k, v, i_pre, log_f):
    bsz = q.shape[0]
    qc, kc, vc = to_chunks(q, MLSTM_CHUNK), to_chunks(k, MLSTM_CHUNK), to_chunks(v, MLSTM_CHUNK)
    ic = to_chunks(i_pre, MLSTM_CHUNK)
    bc = lax.cumsum(to_chunks(log_f, MLSTM_CHUNK), axis=3)
    mask = jnp.tril(jnp.ones((MLSTM_CHUNK, MLSTM_CHUNK), dtype=bool))

    def body(carry, inp):
        c_prev, n_prev, m_prev = carry
        qb, kb, vb, ib, bb = inp
        log_d = jnp.where(mask, bb[..., :, None] - bb[..., None, :] + ib[..., None, :], -jnp.inf)
        inter_log = bb + m_prev[..., None]
        m = jnp.maximum(inter_log, jnp.max(log_d, axis=-1))
        g_inter = jnp.exp(inter_log - m)
        scores = jnp.einsum('bhtd,bhsd->bhts', qb, kb) * jnp.exp(log_d - m[..., None])
        num = (jnp.einsum('bhts,bhsv->bhtv', scores, vb)
               + g_inter[..., None] * jnp.einsum('bhtd,bhdv->bhtv', qb, c_prev))
        den = jnp.sum(scores, axis=-1) + g_inter * jnp.einsum('bhtd,bhd->bht', qb, n_prev)
        h = num / jnp.maximum(jnp.abs(den), jnp.exp(-m))[..., None]
        b_last = bb[..., -1]
        log_w = b_last[..., None] - bb + ib
        m_new = jnp.maximum(b_last + m_prev, jnp.max(log_w, axis=-1))
        w = jnp.exp(log_w - m_new[..., None])
        decay = jnp.exp(b_last + m_prev - m_new)
        c_new = decay[..., None, None] * c_prev + jnp.einsum('bhs,bhsd,bhsv->bhdv', w, kb, vb)
        n_new = decay[..., None] * n_prev + jnp.einsum('bhs,bhsd->bhd', w, kb)
        return (c_new, n_new, m_new), h

    carry0 = (jnp.zeros((bsz, MLSTM_HEADS, MLSTM_DK, MLSTM_DV), jnp.float32),
              jnp.zeros((bsz, MLSTM_HEADS, MLSTM_DK), jnp.float32),
              jnp.zeros((bsz, MLSTM_HEADS), jnp.float32))
    _, out = lax.scan(body, carry0, (qc, kc, vc, ic, bc))
    return from_chunks(out)


def causal_dwconv(x, w, b):
    y = lax.conv_general_dilated(
        x, w[:, None, :].astype(x.dtype), window_strides=(1,),
        padding=[(CONV_WIDTH - 1, 0)], dimension_numbers=('NWC', 'WIO', 'NWC'),
        feature_group_count=x.shape[-1])
    return y + b.astype(x.dtype)


def hybrid_mixer(n, w_in, gla_gate_w2, gla_gate_b, gla_norm_g, mlstm_conv_w, mlstm_conv_b,
                 mlstm_igate_b, mlstm_fgate_b, mlstm_norm_g, w_out):
    bsz, s, _ = n.shape
    f32 = jnp.float32
    proj = n @ w_in
    offsets = np.cumsum(SPLIT_SIZES)[:-1].tolist()
    qg, kg, vg, gg, ag, qm, km, vm, om, im, fm = jnp.split(proj, offsets, axis=-1)

    qg = qg.astype(f32).reshape(bsz, s, GLA_HEADS, GLA_DK) * (GLA_DK ** -0.5)
    kg = kg.astype(f32).reshape(bsz, s, GLA_HEADS, GLA_DK)
    vg = vg.astype(f32).reshape(bsz, s, GLA_HEADS, GLA_DV)
    gate_logit = (ag @ gla_gate_w2 + gla_gate_b).astype(f32)
    log_a = (jax.nn.log_sigmoid(gate_logit) / GLA_TAU).reshape(bsz, s, GLA_HEADS, GLA_DK)
    o_gla = gla_chunked(qg, kg, vg, log_a)
    o_gla = head_rmsnorm(o_gla, gla_norm_g) * jax.nn.silu(
        gg.astype(f32)).reshape(bsz, s, GLA_HEADS, GLA_DV)
    o_gla = o_gla.reshape(bsz, s, GROUP_WIDTH)

    qk = jax.nn.silu(causal_dwconv(jnp.concatenate([qm, km], axis=-1), mlstm_conv_w, mlstm_conv_b))
    qm, km = jnp.split(qk.astype(f32), 2, axis=-1)
    qm = qm.reshape(bsz, s, MLSTM_HEADS, MLSTM_DK)
    km = km.reshape(bsz, s, MLSTM_HEADS, MLSTM_DK) * (MLSTM_DK ** -0.5)
    vm = vm.astype(f32).reshape(bsz, s, MLSTM_HEADS, MLSTM_DV)
    i_pre = (im + mlstm_igate_b).astype(f32)
    log_f = jax.nn.log_sigmoid((fm + mlstm_fgate_b).astype(f32))
    h = mlstm_chunked(qm, km, vm, i_pre, log_f)
    o_ml = jax.nn.sigmoid(om.astype(f32)).reshape(bsz, s, MLSTM_HEADS, MLSTM_DV) * head_rmsnorm(h, mlstm_norm_g)
    o_ml = o_ml.reshape(bsz, s, GROUP_WIDTH)

    mixed = jnp.concatenate([o_gla, o_ml], axis=-1).astype(n.dtype)
    return mixed @ w_out


def cross_attention(n, mem_n, wq, wk, wv, wo):
    bsz, s, _ = n.shape
    m = mem_n.shape[1]
    q = (n @ wq).reshape(bsz, s, XATTN_HEADS, XATTN_HEAD_DIM)
    k = (mem_n @ wk).reshape(bsz, m, XATTN_HEADS, XATTN_HEAD_DIM)
    v = (mem_n @ wv).reshape(bsz, m, XATTN_HEADS, XATTN_HEAD_DIM)
    scores = jnp.einsum('bshd,bmhd->bhsm', q, k).astype(jnp.float32) * (XATTN_HEAD_DIM ** -0.5)
    p = jax.nn.softmax(scores, axis=-1).astype(v.dtype)
    o = jnp.einsum('bhsm,bmhd->bshd', p, v).reshape(bsz, s, D_MODEL)
    return o @ wo


def swiglu(n, w_gate, w_up, w_down):
    return (jax.nn.silu(n @ w_gate) * (n @ w_up)) @ w_down


def setup_inputs(seed: int = 0) -> dict:
    key = jax.random.key(seed)
    ks = jax.random.split(key, 26)
    f32 = jnp.float32

    def dense(k, shape, fan_in):
        return jax.random.normal(k, shape, f32) * (fan_in ** -0.5)

    def gain(k, shape):
        return 1.0 + 0.01 * jax.random.normal(k, shape, f32)

    def small(k, shape):
        return 0.01 * jax.random.normal(k, shape, f32)

    L = DEPTH
    qk_ch = 2 * MLSTM_HEADS * MLSTM_DK
    return {
        "x": jax.random.normal(ks[0], (BATCH, SEQ, D_MODEL), f32),
        "mem": jax.random.normal(ks[1], (BATCH, MEM_TOKENS, D_MODEL), f32),
        "norm_mix_g": gain(ks[2], (L, D_MODEL)),
        "w_in": dense(ks[3], (L, D_MODEL, IN_COLS), D_MODEL),
        "gla_gate_w2": dense(ks[4], (L, GLA_GATE_RANK, GLA_HEADS * GLA_DK), GLA_GATE_RANK),
        "gla_gate_b": small(ks[5], (L, GLA_HEADS * GLA_DK)),
        "gla_norm_g": gain(ks[6], (L, GLA_HEADS, GLA_DV)),
        "mlstm_conv_w": dense(ks[7], (L, CONV_WIDTH, qk_ch), CONV_WIDTH),
        "mlstm_conv_b": small(ks[8], (L, qk_ch)),
        "mlstm_igate_b": small(ks[9], (L, MLSTM_HEADS)),
        "mlstm_fgate_b": jnp.linspace(3.0, 6.0, MLSTM_HEADS, dtype=f32)[None, :]
                          + 0.1 * jax.random.normal(ks[10], (L, MLSTM_HEADS), f32),
        "mlstm_norm_g": gain(ks[11], (L, MLSTM_HEADS, MLSTM_DV)),
        "w_out": dense(ks[12], (L, MIX_WIDTH, D_MODEL), MIX_WIDTH),
        "norm_cross_g": gain(ks[13], (L, D_MODEL)),
        "norm_mem_g": gain(ks[14], (L, D_MODEL)),
        "wq_c": dense(ks[15], (L, D_MODEL, D_MODEL), D_MODEL),
        "wk_c": dense(ks[16], (L, D_MODEL, D_MODEL), D_MODEL),
        "wv_c": dense(ks[17], (L, D_MODEL, D_MODEL), D_MODEL),
        "wo_c": dense(ks[18], (L, D_MODEL, D_MODEL), D_MODEL),
        "norm_ffn_g": gain(ks[19], (L, D_MODEL)),
        "w_gate": dense(ks[20], (L, D_MODEL, FFN_HIDDEN), D_MODEL),
        "w_up": dense(ks[21], (L, D_MODEL, FFN_HIDDEN), D_MODEL),
        "w_down": dense(ks[22], (L, FFN_HIDDEN, D_MODEL), FFN_HIDDEN),
        "norm_final_g": gain(ks[23], (D_MODEL,)),
    }


def reference(x, mem, norm_mix_g, w_in, gla_gate_w2, gla_gate_b, gla_norm_g, mlstm_conv_w,
              mlstm_conv_b, mlstm_igate_b, mlstm_fgate_b, mlstm_norm_g, w_out, norm_cross_g,
              norm_mem_g, wq_c, wk_c, wv_c, wo_c, norm_ffn_g, w_gate, w_up, w_down, norm_final_g):
    h = x
    for l in range(DEPTH):
        h = h + hybrid_mixer(rmsnorm(h, norm_mix_g[l]), w_in[l], gla_gate_w2[l], gla_gate_b[l],
                             gla_norm_g[l], mlstm_conv_w[l], mlstm_conv_b[l], mlstm_igate_b[l],
                             mlstm_fgate_b[l], mlstm_norm_g[l], w_out[l])
        h = h + cross_attention(rmsnorm(h, norm_cross_g[l]), rmsnorm(mem, norm_mem_g[l]),
                                wq_c[l], wk_c[l], wv_c[l], wo_c[l])
        h = h + swiglu(rmsnorm(h, norm_ffn_g[l]), w_gate[l], w_up[l], w_down[l])
    return rmsnorm(h, norm_final_g)
```

```python
import numpy as np
import concourse.bass as bass
import concourse.mybir as mybir
from concourse.bass_utils import run_bass_kernel_spmd

F32 = mybir.dt.float32
BF16 = mybir.dt.bfloat16
AF = mybir.ActivationFunctionType
ALU = mybir.AluOpType

D = 4096
T = 512
EPS = 1e-6
NSLOT = 4
SLOT = 8192
FH = 11008
HID_BLOCKS = [(0, 22), (22, 22), (44, 21), (65, 21)]

C_ID, C_UG, C_UN, C_TRI, C_ONE = 0, 128, 256, 384, 512
C_GMIX, C_GCR, C_GFFN, C_GMEM, C_GNORM = 640, 672, 704, 736, 768
C_CW, C_CB, C_BIF, C_FLAG = 800, 864, 880, 888
NCST = 904
LN16 = float(np.log(1.0 / 16.0))


class Prog:
    EPOCH = 30000
    KDMA = 8

    def __init__(self, nc):
        self.nc = nc
        self.eng = {'pe': nc.tensor, 'act': nc.scalar, 'dve': nc.vector, 'pool': nc.gpsimd, 'sp': nc.sync}
        self.ops = []
        self.last_w = {}
        self.readers = {}
        self.rotc = {}

    def op(self, eng, fn, r=(), w=(), dma=False):
        i = len(self.ops)
        deps = set()
        for k in r:
            lw = self.last_w.get(k)
            if lw is not None:
                deps.add(lw)
        for k in w:
            lw = self.last_w.get(k)
            if lw is not None:
                deps.add(lw)
            rd = self.readers.get(k)
            if rd:
                deps.update(rd[0].values())
                deps.update(rd[1])
        for k in r:
            rd = self.readers.get(k)
            if rd is None:
                rd = self.readers[k] = ({}, [])
            if dma:
                rd[1].append(i)
            else:
                rd[0][eng] = i
        for k in w:
            self.last_w[k] = i
            self.readers[k] = ({}, [])
        deps.discard(i)
        latest = {}
        fd = []
        for d in deps:
            o = self.ops[d]
            if o['dma']:
                fd.append(d)
                continue
            if o['eng'] == 'pe' and eng == 'pe':
                continue
            if o['eng'] not in latest or latest[o['eng']] < d:
                latest[o['eng']] = d
        fd.extend(latest.values())
        self.ops.append({'eng': eng, 'fn': fn, 'deps': fd, 'dma': dma, 'inc': False})
        return i

    def rot(self, grp, names):
        c = self.rotc.get(grp, 0)
        self.rotc[grp] = c + 1
        return names[c % len(names)]

    def emit(self):
        nc = self.nc
        ops = self.ops
        for o in ops:
            for d in o['deps']:
                ops[d]['inc'] = True
        cnt = {e: 0 for e in self.eng}
        dcnt = {e: 0 for e in self.eng}
        sems = {}

        def getsem(name):
            if name not in sems:
                sems[name] = nc.alloc_semaphore(name)
            return sems[name]
        comp = [None] * len(ops)
        pre = [None] * len(ops)
        for i, o in enumerate(ops):
            e = o['eng']
            if o['dma']:
                j = dcnt[e]
                dcnt[e] += 1
                nm = "d_%s_%d" % (e, j % self.KDMA)
                comp[i] = (nm, 16 * (j // self.KDMA + 1))
                if j >= self.KDMA:
                    pre[i] = (nm, 16 * (j // self.KDMA))
            elif o['inc']:
                c = cnt[e]
                cnt[e] += 1
                comp[i] = ("c_%s_%d" % (e, c // self.EPOCH), c % self.EPOCH + 1)
        known = {e: {} for e in self.eng}
        final = {}
        nwait = 0
        for i, o in enumerate(ops):
            e = o['eng']
            h = self.eng[e]
            waits = [comp[d] for d in o['deps']]
            if pre[i] is not None:
                waits.append(pre[i])
            kn = known[e]
            best = {}
            for (nm, v) in waits:
                if kn.get(nm, 0) >= v:
                    continue
                if nm not in best or best[nm] < v:
                    best[nm] = v
            for nm, v in best.items():
                h.wait_ge(getsem(nm), v)
                kn[nm] = v
                nwait += 1
            ins = o['fn'](h)
            if comp[i] is not None:
                nm, v = comp[i]
                ins.then_inc(getsem(nm), 16 if o['dma'] else 1)
                if o['dma']:
                    final[nm] = v
        for nm, v in final.items():
            nc.sync.wait_ge(getsem(nm), v)
        self.stats = {'ops': len(ops), 'waits': nwait, 'cnt': cnt, 'dcnt': dcnt, 'sems': len(sems)}


def build(npre, nown, last_pre_q=True, stop=None, dbg=False):
    nc = bass.Bass("TRN2", target_bir_lowering=False)
    P = Prog(nc)
    NT = npre + nown

    def din(name, shape, dt=F32):
        return nc.dram_tensor(name, shape, dt, kind="ExternalInput").ap()
    xs = din("xs", [NT * T, D])
    mem = din("mem", [256, D])
    w_in = din("w_in", [D, 12312])
    w_out = din("w_out", [D, D])
    wq = din("wq", [D, D])
    wk = din("wk", [D, D])
    wv = din("wv", [D, D])
    wo = din("wo", [D, D])
    w_gate = din("w_gate", [D, FH])
    w_up = din("w_up", [D, FH])
    w_down = din("w_down", [FH, D])
    cst_d = din("cst", [128, NCST])
    w2_d = din("w2aug", [32, 1024])
    gfin_d = din("gfin", [128, D])
    out = nc.dram_tensor("out", [nown * T, D], F32, kind="ExternalOutput").ap()

    def dscr(name, shape, dt):
        return nc.dram_tensor(name, shape, dt, kind="Internal").ap()
    st_gla = dscr("st_gla", [4, 128, 1024], F32)
    st_mC = dscr("st_mC", [4, 128, 1024], F32)
    st_mn = dscr("st_mn", [4, 128, 2], F32)
    KT_d = dscr("KT_d", [32, 128, 256], BF16)
    V_d = dscr("V_d", [2, 128, D], BF16)

    A = nc.alloc_sbuf_tensor("A", [128, 16384], BF16)
    B = nc.alloc_sbuf_tensor("B", [128, 16384], F32)
    Cc = nc.alloc_sbuf_tensor("C", [128, 16384], BF16)
    Wt = [nc.alloc_sbuf_tensor("W%d" % i, [128, SLOT], BF16) for i in range(NSLOT)]
    CST = nc.alloc_sbuf_tensor("CST", [128, NCST], F32)
    W2 = nc.alloc_sbuf_tensor("W2AUG", [32, 1024], F32)
    WAG = nc.alloc_sbuf_tensor("WAG", [128, 512], BF16)
    WIF = nc.alloc_sbuf_tensor("WIF", [128, 256], BF16)
    HAL = nc.alloc_sbuf_tensor("HAL", [128, 48], F32)
    SM = nc.alloc_sbuf_tensor("SM", [128, 128], F32)
    ONESB = nc.alloc_sbuf_tensor("ONESB", [128, 128], BF16)
    AGT = nc.alloc_sbuf_tensor("AGT", [32, 512], F32)
    PS = {n: nc.alloc_psum_tensor(n, [128, 512], F32) for n in
          ['pj0', 'pj1', 'pt0', 'pt1', 'mA', 'mS', 'mO', 'mU']}

    nT = A[:, :].rearrange("p (k t) -> p k t", k=32)
    Bx = B[:, :].rearrange("p (s c) -> p s c", s=4)
    ident = CST[:, C_ID:C_ID + 128]
    Ugla = CST[:, C_UG:C_UG + 128]
    Uneg = CST[:, C_UN:C_UN + 128]
    tri = CST[:, C_TRI:C_TRI + 128]
    onesf = CST[:, C_ONE:C_ONE + 128]

    def mm(out_, lhsT, rhs, start, stop, r, w):
        return P.op('pe', lambda e, a=out_, b=lhsT, c=rhs, s=start, t=stop: e.matmul(a, lhsT=b, rhs=c, start=s, stop=t), r, w)

    def tr(out_, in_, r, w):
        return P.op('pe', lambda e, a=out_, b=in_: e.transpose(a, b, ident), list(r) + ['cst'], w)

    def act(out_, in_, func, r, w, bias=None, scale=None, accum=None):
        kw = {}
        if bias is not None:
            kw['bias'] = bias
        if scale is not None:
            kw['scale'] = scale
        if accum is not None:
            kw['accum_out'] = accum
        return P.op('act', lambda e, a=out_, b=in_, f=func, k=kw: e.activation(out=a, in_=b, func=f, **k), r, w)

    def tt(out_, in0, in1, op_, r, w, eng='dve'):
        return P.op(eng, lambda e, a=out_, b=in0, c=in1, o=op_: e.tensor_tensor(out=a, in0=b, in1=c, op=o), r, w)

    def ts(out_, in0, s1, s2, op0, op1, r, w, eng='dve'):
        if s2 is None:
            return P.op(eng, lambda e, a=out_, b=in0, c=s1, o=op0: e.tensor_single_scalar(out=a, in_=b, scalar=c, op=o), r, w)
        return P.op(eng, lambda e, a=out_, b=in0, c=s1, d=s2, o=op0, q=op1: e.tensor_scalar(out=a, in0=b, scalar1=c, scalar2=d, op0=o, op1=q), r, w)

    def stt(out_, in0, sc, in1, op0, op1, r, w, eng='dve'):
        return P.op(eng, lambda e, a=out_, b=in0, c=sc, d=in1, o=op0, q=op1: e.scalar_tensor_tensor(out=a, in0=b, scalar=c, in1=d, op0=o, op1=q), r, w)

    def cp(out_, in_, r, w, eng='dve'):
        if eng == 'act':
            return P.op(eng, lambda e, a=out_, b=in_: e.activation(out=a, in_=b, func=AF.Copy), r, w)
        return P.op(eng, lambda e, a=out_, b=in_: e.tensor_copy(out=a, in_=b), r, w)

    def ms(ap, val, w, eng='dve'):
        return P.op(eng, lambda e, a=ap, v=val: e.memset(a, v), [], w)

    def dma(q, out_, in_, r, w):
        return P.op(q, lambda e, a=out_, b=in_: e.dma_start(out=a, in_=b), r, w, dma=True)

    def dump(name, ap, keys):
        shp = list(ap.shape)
        d = nc.dram_tensor("dbg_" + name, shp, F32, kind="ExternalOutput").ap()
        dma('pool', d, ap, keys, [('dbg', name)])

    def load_w(src, a, b):
        c = P.rotc.get('wslot', 0)
        P.rotc['wslot'] = c + 1
        si = c % NSLOT
        view = Wt[si][:, 0:a * b].rearrange("p (a b) -> p a b", a=a)
        key = ('W', si)
        dma('pool', view, src, [], [key])
        return view, key

    def bc3(ap2, n):
        return ap2.unsqueeze(2).broadcast_to([128, ap2.shape[1], n])

    dma('sp', CST[:, :], cst_d[:, :], [], ['cst'])
    dma('sp', W2[:, :], w2_d[:, :], [], ['w2'])
    dma('pool', WAG[:, :].rearrange("p (k n) -> p k n", k=32),
        w_in[:, 6144:6160].rearrange("(k p) n -> p k n", p=128), [], ['wag'])
    dma('pool', WIF[:, :].rearrange("p (k n) -> p k n", k=32),
        w_in[:, 12304:12312].rearrange("(k p) n -> p k n", p=128), [], ['wif'])
    ms(HAL[:, :], 0.0, ['hal'])
    ms(AGT[:, :], 1.0, ['agt'])
    ms(ONESB[:, :], 1.0, ['cst'])
    WAG3 = WAG[:, :].rearrange("p (k n) -> p k n", k=32)
    WIF3 = WIF[:, :].rearrange("p (k n) -> p k n", k=32)

    junk = Cc[:, 0:4096]
    gfin_sb = Cc[:, 4096:12288].bitcast(F32)
    XN = [Cc[:, 12288 + i * 1024:12288 + (i + 1) * 1024].bitcast(F32) for i in range(2)]
    mixedT = Cc[:, :].rearrange("p (k t) -> p k t", k=32)

    def rmsnorm_T(nst, src, skeys, gcol, ncols_tok=512):
        for st in range(nst):
            ssv = SM[:, 2 * st:2 * st + 1]
            rsv = SM[:, 2 * st + 1:2 * st + 2]
            ms(ssv, 0.0, [('ss', st)])
            act(junk, src(st), AF.Square, skeys(st), [('ss', st)], accum=ssv)
            ts(rsv, ssv, 1.0 / D, EPS, ALU.mult, ALU.add, [('ss', st)], [('rs', st)])
            act(rsv, rsv, AF.Ln, [('rs', st)], [('rs', st)])
            act(rsv, rsv, AF.Exp, [('rs', st)], [('rs', st)], scale=-0.5)
            for cb in range(8):
                xn = XN[cb % 2]
                act(xn, src(st)[:, cb * 512:(cb + 1) * 512], AF.Copy, list(skeys(st)) + [('rs', st)], [('xn', cb % 2)], scale=rsv)
                pb = P.rot('pt', ['pt0', 'pt1'])
                for j in range(4):
                    tr(PS[pb][:, j * 128:(j + 1) * 128], xn[:, j * 128:(j + 1) * 128], [('xn', cb % 2)], [pb])
                tt(nT[:, cb * 4:(cb + 1) * 4, st * 128:(st + 1) * 128],
                   PS[pb][:, :].rearrange("p (a b) -> p a b", a=4),
                   bc3(CST[:, gcol + cb * 4:gcol + cb * 4 + 4], 128), ALU.mult,
                   [pb, 'cst'], [('nT', st)])

    def bkeys(st, c0=0, c1=16):
        return [('B', st, c) for c in range(c0, c1)]

    NTK = [('nT', s) for s in range(4)]

    def proj_fm(wd, c0, nch, evac, ntok=512, nk=None):
        nk = nk or NTK
        m = 0
        while m < nch:
            nm = min(2, nch - m)
            slot, skey = load_w(wd[:, c0 + m * 128:c0 + (m + nm) * 128].rearrange("(k p) n -> p k n", p=128), 32, nm * 128)
            for mi in range(nm):
                pb = P.rot('pj', ['pj0', 'pj1'])
                for kc in range(32):
                    mm(PS[pb][:, 0:ntok], slot[:, kc, mi * 128:(mi + 1) * 128], nT[:, kc, 0:ntok], kc == 0, kc == 31,
                       [skey] + nk, [pb])
                evac(m + mi, PS[pb][:, 0:ntok], pb)
            m += nm

    def proj_tm(wd, c0, npiece, evac, nst=4):
        for pc in range(npiece):
            slot, skey = load_w(wd[:, c0 + pc * 256:c0 + (pc + 1) * 256].rearrange("(k p) n -> p k n", p=128), 32, 256)
            for st in range(nst):
                pb = P.rot('pj', ['pj0', 'pj1'])
                for kc in range(32):
                    mm(PS[pb][:, 0:256], nT[:, kc, st * 128:(st + 1) * 128], slot[:, kc, :], kc == 0, kc == 31,
                       [skey, ('nT', st)], [pb])
                evac(pc, st, PS[pb][:, 0:256], pb)

    def phase_mem():
        for st in range(2):
            dma('sp', Bx[:, st, :], mem[st * 128:(st + 1) * 128, :], [], bkeys(st))
        rmsnorm_T(2, lambda st: Bx[:, st, :], lambda st: bkeys(st), C_GMEM)
        ktb = [Cc[:, 14336 + i * 256:14336 + (i + 1) * 256] for i in range(2)]

        def ev_k(m, ps, pb):
            i = m % 2
            cp(ktb[i], ps, [pb], [('ktb', i)], eng='act')
            dma('sp', KT_d[m], ktb[i], [('ktb', i)], ['KT_d'])
        proj_fm(wk, 0, 32, ev_k, ntok=256, nk=[('nT', 0), ('nT', 1)])
        vtb = [Cc[:, 14848 + i * 256:14848 + (i + 1) * 256] for i in range(2)]

        def ev_v(pc, st, ps, pb):
            i = (pc * 2 + st) % 2
            cp(vtb[i], ps, [pb], [('vtb', i)], eng='dve')
            dma('sp', V_d[st][:, pc * 256:(pc + 1) * 256], vtb[i], [('vtb', i)], ['V_d'])
        proj_tm(wv, 0, 16, ev_v, nst=2)

    def bsub(off, n):
        return B[:, off:off + n]
    qT_f = bsub(0, 1024).rearrange("p (a b) -> p a b", a=2)
    kT_f = bsub(1024, 1024).rearrange("p (a b) -> p a b", a=2)
    v_bf = bsub(2048, 1024).bitcast(BF16).rearrange("p (a b) -> p a b", a=4)
    gs_f = bsub(3072, 2048).rearrange("p (a b) -> p a b", a=4)
    sp_f = bsub(5120, 1024).rearrange("p (a b) -> p a b", a=4)
    qk_bf = bsub(5120, 1024).bitcast(BF16)
    q_bf = qk_bf[:, 0:1024].rearrange("p (a b) -> p a b", a=2)
    k_bf = qk_bf[:, 1024:2048].rearrange("p (a b) -> p a b", a=2)
    xq = bsub(6144, 1030).rearrange("p (a b) -> p a b", a=2)
    xk = bsub(7174, 1030).rearrange("p (a b) -> p a b", a=2)
    R0 = 8208
    E1 = bsub(R0, 256).rearrange("p (a b) -> p a b", a=2)
    E2 = bsub(R0 + 256, 256).rearrange("p (a b) -> p a b", a=2)
    qtl = bsub(R0 + 512, 128).bitcast(BF16).rearrange("p (a b) -> p a b", a=2)
    ktl = bsub(R0 + 640, 128).bitcast(BF16).rearrange("p (a b) -> p a b", a=2)
    khT = bsub(R0 + 768, 256).rearrange("p (a b) -> p a b", a=2)
    kh_bf = bsub(R0 + 1024, 128).bitcast(BF16)
    AT_bf = bsub(R0 + 1152, 64).bitcast(BF16)
    og = bsub(R0 + 1216, 512)
    S_f = bsub(R0 + 1728, 1024).rearrange("p (a b) -> p a b", a=2)
    S_bf = bsub(R0 + 2752, 512).bitcast(BF16).rearrange("p (a b) -> p a b", a=2)
    Lf = bsub(R0 + 3264, 128)
    DT_f = bsub(R0 + 3392, 128)
    EB = bsub(R0 + 3520, 128)
    elog = bsub(R0 + 3648, 256)
    n_f = bsub(R0 + 3904, 2)
    n_bf = bsub(R0 + 3906, 1).bitcast(BF16)
    sm2 = bsub(R0 + 3908, 16)
    gi_sb = bsub(R0 + 3924, 32).rearrange("p (a b) -> p a b", a=4)
    spf = bsub(R0 + 3956, 16).rearrange("p (a b) -> p a b", a=4)
    colb = bsub(R0 + 3972, 16).rearrange("p (a b) -> p a b", a=4)
    colbD = bsub(R0 + 3988, 16).rearrange("p (a b) -> p a b", a=4)
    ef_t = bsub(R0 + 4004, 4)
    wexp = bsub(R0 + 4008, 1)
    nflag = bsub(R0 + 4009, 2)
    junk2 = bsub(R0 + 4020, 256).bitcast(BF16)

    def gnorm_bc(c4):
        return bc3(CST[:, C_GNORM + c4:C_GNORM + c4 + 4], 128)

    def out_epilogue(h8, st, rtot):
        pb = P.rot('pt', ['pt0', 'pt1'])
        for j in range(4):
            tr(PS[pb][:, j * 128:(j + 1) * 128], og[:, j * 128:(j + 1) * 128], ['og'], [pb])
        tt(mixedT[:, h8 * 4:(h8 + 1) * 4, st * 128:(st + 1) * 128],
           PS[pb][:, :].rearrange("p (a b) -> p a b", a=4), gnorm_bc(h8 * 4), ALU.mult,
           [pb, 'cst'], [('mixT', h8)])

    def gates_tile(full):
        pb = P.rot('pj', ['pj0', 'pj1'])
        for kc in range(32):
            mm(PS[pb][0:16, :], WAG3[:, kc, :], nT[:, kc, :], kc == 0, kc == 31, ['wag'] + NTK, [pb])
        cp(AGT[0:16, :], PS[pb][0:16, :], [pb], ['agt'], eng='act')
        for st in range(4):
            pb = P.rot('pj', ['pj0', 'pj1'])
            for kc in range(32):
                mm(PS[pb][:, 0:8], nT[:, kc, st * 128:(st + 1) * 128], WIF3[:, kc, :], kc == 0, kc == 31,
                   ['wif', ('nT', st)], [pb])
            tt(gi_sb[:, st, :], PS[pb][:, 0:8], CST[:, C_BIF:C_BIF + 8], ALU.add, [pb, 'cst'], ['gi'])
            act(ef_t, gi_sb[:, st, 4:8], AF.Exp, ['gi'], ['ef'], scale=-1.0)
            act(spf[:, st, :], ef_t, AF.Ln, ['ef'], ['spf'], bias=1.0)
            mm(PS['mU'][:, 0:4], Uneg, spf[:, st, :], True, True, ['cst', 'spf'], ['mU'])
            tt(colb[:, st, :], gi_sb[:, st, 0:4], PS['mU'][:, 0:4], ALU.subtract, ['gi', 'mU'], ['colb'])
            ts(colbD[:, st, :], colb[:, st, :], LN16, None, ALU.add, None, ['colb'], ['colbD'])

    def gla_head(h, ti, full):
        first = (ti == 0)
        def ev_k(m, ps, pb):
            cp(kT_f[:, m, :], ps, [pb], ['kT_f'], eng='act')
        proj_fm(w_in, 1024 + h * 256, 2, ev_k)

        def ev_v(pc, st, ps, pb):
            cp(v_bf[:, st, pc * 256:(pc + 1) * 256], ps, [pb], [('v_bf', st)], eng='dve')
        proj_tm(w_in, 2048 + h * 512, 2, ev_v)
        if full:
            def ev_q(m, ps, pb):
                act(qT_f[:, m, :], ps, AF.Copy, [pb], ['qT_f'], scale=1.0 / 16.0)
            proj_fm(w_in, h * 256, 2, ev_q)

            def ev_g(pc, st, ps, pb):
                act(gs_f[:, st, pc * 256:(pc + 1) * 256], ps, AF.Silu, [pb], [('gs', st)])
            proj_tm(w_in, 4096 + h * 512, 2, ev_g)
        if first:
            ms(S_f[:, :, :], 0.0, ['S_f'])
        else:
            dma('sp', S_f[:, :, :], st_gla[h].rearrange("p (a b) -> p a b", a=2), [('st_gla', h)], ['S_f'])
        cp(S_bf[:, :, :], S_f[:, :, :], ['S_f'], ['S_bf'], eng='act')
        for st in range(4):
            mm(PS['mA'][:, 0:256], AGT[0:32, st * 128:(st + 1) * 128], W2[0:32, h * 256:(h + 1) * 256], True, True,
               ['agt', 'w2'], ['mA'])
            act(elog, PS['mA'][:, 0:256], AF.Exp, ['mA'], ['elog'], scale=-1.0)
            act(sp_f[:, st, :], elog, AF.Ln, ['elog'], ['sp'], bias=1.0)
            for dc in range(2):
                mm(PS['mA'][:, dc * 128:(dc + 1) * 128], sp_f[:, st, dc * 128:(dc + 1) * 128], Ugla, True, True,
                   ['sp', 'cst'], ['mA'])
            pbT = PS['mA'][:, 0:256].rearrange("p (a b) -> p a b", a=2)
            act(E1[:, :, :], pbT, AF.Exp, ['mA'], ['E1'])
            act(E2[:, :, :], pbT, AF.Exp, ['mA'], ['E2'], scale=-1.0)
            ksl = kT_f[:, :, st * 128:(st + 1) * 128]
            tt(ktl[:, :, :], ksl, E2[:, :, :], ALU.mult, ['kT_f', 'E2'], ['ktl'])
            for dc in range(2):
                stt(khT[:, dc, :], kT_f[:, dc, st * 128:(st + 1) * 128], E1[:, dc, 127:128], E2[:, dc, :],
                    ALU.mult, ALU.mult, ['kT_f', 'E1', 'E2'], ['khT'])
            pb = P.rot('pt', ['pt0', 'pt1'])
            for dc in range(2):
                tr(PS[pb][:, dc * 128:(dc + 1) * 128], khT[:, dc, :], ['khT'], [pb])
            cp(kh_bf[:, :], PS[pb][:, 0:256], [pb], ['kh_bf'], eng='act')
            if dbg and ti == 0 and h == 0 and st == 0:
                dump("agt", AGT[0:32, 0:128], ['agt'])
                dump("sp", sp_f[:, 0, :], ['sp'])
                dump("E1", E1[:, :, :], ['E1'])
                dump("E2", E2[:, :, :], ['E2'])
                dump("kT", kT_f[:, :, 0:128], ['kT_f'])
                dump("khT", khT[:, :, :], ['khT'])
                dump("kh", kh_bf[:, :], ['kh_bf'])
                dump("v", v_bf[:, 0, :], [('v_bf', 0)])
            if full:
                tt(qtl[:, :, :], qT_f[:, :, st * 128:(st + 1) * 128], E1[:, :, :], ALU.mult, ['qT_f', 'E1'], ['qtl'])
                for dc in range(2):
                    mm(PS['mS'][:, 0:128], ktl[:, dc, :], qtl[:, dc, :], dc == 0, dc == 1, ['ktl', 'qtl'], ['mS'])
                tt(AT_bf[:, :], PS['mS'][:, 0:128], tri, ALU.mult, ['mS', 'cst'], ['AT'])
                mm(PS['mO'][:, :], AT_bf[:, :], v_bf[:, st, :], True, False, ['AT', ('v_bf', st)], ['mO'])
                for dc in range(2):
                    mm(PS['mO'][:, :], qtl[:, dc, :], S_bf[:, dc, :], False, dc == 1, ['qtl', 'S_bf'], ['mO'])
                ssq = sm2[:, 0:1]
                rs = sm2[:, 1:2]
                ms(ssq, 0.0, ['ssq'])
                act(junk2, PS['mO'][:, :], AF.Square, ['mO'], ['ssq'], accum=ssq)
                ts(rs, ssq, 1.0 / 512, EPS, ALU.mult, ALU.add, ['ssq'], ['rsq'])
                act(rs, rs, AF.Ln, ['rsq'], ['rsq'])
                act(rs, rs, AF.Exp, ['rsq'], ['rsq'], scale=-0.5)
                stt(og, PS['mO'][:, :], rs, gs_f[:, st, :], ALU.mult, ALU.mult, ['mO', 'rsq', ('gs', st)], ['og'])
                out_epilogue(h, st, None)
            for dc in range(2):
                pb2 = ['pj0', 'pj1'][dc]
                mm(PS[pb2][:, :], kh_bf[:, dc * 128:(dc + 1) * 128], v_bf[:, st, :], True, True,
                   ['kh_bf', ('v_bf', st)], [pb2])
                stt(S_f[:, dc, :], S_f[:, dc, :], E1[:, dc, 127:128], PS[pb2][:, :], ALU.mult, ALU.add,
                    ['S_f', 'E1', pb2], ['S_f'])
            cp(S_bf[:, :, :], S_f[:, :, :], ['S_f'], ['S_bf'], eng='act')
            if dbg and ti == 0 and h == 0 and st == 0:
                dump("S1", S_f[:, :, :], ['S_f'])
        dma('sp', st_gla[h].rearrange("p (a b) -> p a b", a=2), S_f[:, :, :], ['S_f'], [('st_gla', h)])

    def conv_silu(xin, ch0, outf, key_in, key_out):
        for dc in range(2):
            ch = ch0 + dc
            wcol = C_CW + ch * 4
            ts(outf[:, dc, :], xin[:, dc, 0:512], CST[:, wcol:wcol + 1], CST[:, C_CB + ch:C_CB + ch + 1],
               ALU.mult, ALU.add, [key_in, 'cst'], [key_out])
            for j in range(1, 4):
                stt(outf[:, dc, :], xin[:, dc, j:j + 512], CST[:, wcol + j:wcol + j + 1], outf[:, dc, :],
                    ALU.mult, ALU.add, [key_in, 'cst', key_out], [key_out])
        act(outf[:, :, :], outf[:, :, :], AF.Silu, [key_out], [key_out])

    def halo_in(xin, ch0, key):
        cp(xin[:, :, 0:3], HAL[:, ch0 * 3:(ch0 + 2) * 3].rearrange("p (a b) -> p a b", a=2), ['hal'], [key], eng='dve')

    def halo_out(xin, ch0, key):
        cp(HAL[:, ch0 * 3:(ch0 + 2) * 3].rearrange("p (a b) -> p a b", a=2), xin[:, :, 512:515], [key], ['hal'], eng='dve')

    def mlstm_head(h, ti, full, want_q_halo, pslot):
        first = (ti == 0)
        km_f = kT_f
        qm_f = qT_f
        halo_in(xk, 8 + h * 2, 'xk')

        def ev_k(m, ps, pb):
            cp(xk[:, m, 3:515], ps, [pb], ['xk'], eng='act')
        proj_fm(w_in, 7184 + h * 256, 2, ev_k)
        conv_silu(xk, 8 + h * 2, km_f, 'xk', 'kT_f')
        halo_out(xk, 8 + h * 2, 'xk')
        cp(k_bf[:, :, :], km_f[:, :, :], ['kT_f'], ['k_bf'], eng='dve')

        def ev_v(pc, st, ps, pb):
            cp(v_bf[:, st, pc * 256:(pc + 1) * 256], ps, [pb], [('v_bf', st)], eng='dve')
        proj_tm(w_in, 8208 + h * 512, 2, ev_v)
        if full or want_q_halo:
            halo_in(xq, h * 2, 'xq')

            def ev_q(m, ps, pb):
                cp(xq[:, m, 3:515], ps, [pb], ['xq'], eng='act')
            proj_fm(w_in, 6160 + h * 256, 2, ev_q)
            if full:
                conv_silu(xq, h * 2, qm_f, 'xq', 'qT_f')
                cp(q_bf[:, :, :], qm_f[:, :, :], ['qT_f'], ['q_bf'], eng='dve')
            halo_out(xq, h * 2, 'xq')
        if full:
            def ev_o(pc, st, ps, pb):
                act(gs_f[:, st, pc * 256:(pc + 1) * 256], ps, AF.Sigmoid, [pb], [('gs', st)])
            proj_tm(w_in, 10256 + h * 512, 2, ev_o)
        if first:
            ms(S_f[:, :, :], 0.0, ['S_f'])
            ms(n_f, 0.0, ['n_f'])
        else:
            dma('sp', S_f[:, :, :], st_mC[h].rearrange("p (a b) -> p a b", a=2), [('st_mC', h)], ['S_f'])
            dma('sp', n_f, st_mn[h], [('st_mn', h)], ['n_f'])
        cp(S_bf[:, :, :], S_f[:, :, :], ['S_f'], ['S_bf'], eng='act')
        cp(n_bf, n_f, ['n_f'], ['n_bf'], eng='dve')
        for st in range(4):
            ts(Lf, onesf, spf[:, st, h:h + 1], None, ALU.mult, None, ['cst', 'spf'], ['Lf'])
            mm(PS['mA'][:, 0:128], Lf, Uneg, True, True, ['Lf', 'cst'], ['mA'])
            act(EB, PS['mA'][:, 0:128], AF.Exp, ['mA'], ['EB'])
            act(wexp, PS['mA'][:, 127:128], AF.Exp, ['mA', 'colb'], ['wexp'], bias=colb[:, st, h:h + 1])
            pb = P.rot('pt', ['pt0', 'pt1'])
            for dc in range(2):
                tr(PS[pb][:, dc * 128:(dc + 1) * 128], km_f[:, dc, st * 128:(st + 1) * 128], ['kT_f'], [pb])
            ts(kh_bf[:, :], PS[pb][:, 0:256], wexp, None, ALU.mult, None, [pb, 'wexp'], ['kh_bf'])
            if full:
                act(DT_f, PS['mA'][:, 0:128], AF.Exp, ['mA', 'colbD'], ['DT'], bias=colbD[:, st, h:h + 1])
                tt(DT_f, DT_f, tri, ALU.mult, ['DT', 'cst'], ['DT'])
                for dc in range(2):
                    stt(qtl[:, dc, :], qm_f[:, dc, st * 128:(st + 1) * 128], 1.0 / 16.0, EB, ALU.mult, ALU.mult,
                        ['qT_f', 'EB'], ['qtl'])
                for dc in range(2):
                    mm(PS['mS'][:, 0:128], k_bf[:, dc, st * 128:(st + 1) * 128], q_bf[:, dc, st * 128:(st + 1) * 128],
                       dc == 0, dc == 1, ['k_bf', 'q_bf'], ['mS'])
                tt(AT_bf[:, :], PS['mS'][:, 0:128], DT_f, ALU.mult, ['mS', 'DT'], ['AT'])
                mm(PS['mO'][:, :], AT_bf[:, :], v_bf[:, st, :], True, False, ['AT', ('v_bf', st)], ['mO'])
                for dc in range(2):
                    mm(PS['mO'][:, :], qtl[:, dc, :], S_bf[:, dc, :], False, dc == 1, ['qtl', 'S_bf'], ['mO'])
                mm(PS['mU'][:, 8:9], AT_bf[:, :], ONESB[:, 0:1], True, False, ['AT', 'cst'], ['mU'])
                for dc in range(2):
                    mm(PS['mU'][:, 8:9], qtl[:, dc, :], n_bf[:, dc:dc + 1], False, dc == 1, ['qtl', 'n_bf'], ['mU'])
                dn = sm2[:, 2:3]
                ssq = sm2[:, 0:1]
                rs = sm2[:, 1:2]
                act(dn, PS['mU'][:, 8:9], AF.Abs, ['mU'], ['dn'])
                ts(dn, dn, 1.0, None, ALU.max, None, ['dn'], ['dn'])
                P.op('dve', lambda e, a=dn: e.reciprocal(out=a, in_=a), ['dn'], ['dn'])
                ms(ssq, 0.0, ['ssq'])
                act(junk2, PS['mO'][:, :], AF.Square, ['mO', 'dn'], ['ssq'], accum=ssq, scale=dn)
                ts(rs, ssq, 1.0 / 512, EPS, ALU.mult, ALU.add, ['ssq'], ['rsq'])
                act(rs, rs, AF.Ln, ['rsq'], ['rsq'])
                act(rs, rs, AF.Exp, ['rsq'], ['rsq'], scale=-0.5)
                tt(rs, rs, dn, ALU.mult, ['rsq', 'dn'], ['rsq'])
                stt(og, PS['mO'][:, :], rs, gs_f[:, st, :], ALU.mult, ALU.mult, ['mO', 'rsq', ('gs', st)], ['og'])
                out_epilogue(4 + h, st, None)
            for dc in range(2):
                pb2 = ['pj0', 'pj1'][dc]
                mm(PS[pb2][:, :], kh_bf[:, dc * 128:(dc + 1) * 128], v_bf[:, st, :], True, True,
                   ['kh_bf', ('v_bf', st)], [pb2])
                stt(S_f[:, dc, :], S_f[:, dc, :], EB[:, 127:128], PS[pb2][:, :], ALU.mult, ALU.add,
                    ['S_f', 'EB', pb2], ['S_f'])
                mm(PS['mU'][:, 16 + dc:17 + dc], kh_bf[:, dc * 128:(dc + 1) * 128], ONESB[:, 0:1], True, True,
                   ['kh_bf', 'cst'], ['mU'])
            if pslot is not None:
                ts(nflag, PS['mU'][:, 16:18], CST[:, C_FLAG + pslot:C_FLAG + pslot + 1], None, ALU.mult, None,
                   ['mU', 'cst'], ['nflag'])
                stt(n_f, n_f, EB[:, 127:128], nflag, ALU.mult, ALU.add, ['n_f', 'EB', 'nflag'], ['n_f'])
            else:
                stt(n_f, n_f, EB[:, 127:128], PS['mU'][:, 16:18], ALU.mult, ALU.add, ['n_f', 'EB', 'mU'], ['n_f'])
            cp(S_bf[:, :, :], S_f[:, :, :], ['S_f'], ['S_bf'], eng='act')
            cp(n_bf, n_f, ['n_f'], ['n_bf'], eng='dve')
        dma('sp', st_mC[h].rearrange("p (a b) -> p a b", a=2), S_f[:, :, :], ['S_f'], [('st_mC', h)])
        dma('sp', st_mn[h], n_f, ['n_f'], [('st_mn', h)])

    def addB(st, c0, n, ps, pb):
        keys = [('B', st, c) for c in range(c0 // 256, (c0 + n) // 256)]
        tt(Bx[:, st, c0:c0 + n], Bx[:, st, c0:c0 + n], ps, ALU.add, keys + [pb], keys)

    def load_x(ti):
        for st in range(4):
            dma('sp', Bx[:, st, :], xs[ti * T + st * 128:ti * T + (st + 1) * 128, :], [], bkeys(st))

    def tile_full(ti, oi):
        fenceC()
        load_x(ti)
        rmsnorm_T(4, lambda st: Bx[:, st, :], lambda st: bkeys(st), C_GMIX)
        fenceB()
        fenceC()
        gates_tile(True)
        for h in range(4):
            gla_head(h, ti, True)
        for h in range(4):
            mlstm_head(h, ti, True, False, None)
        fenceB()
        load_x(ti)
        for cb in range(16):
            slot, skey = load_w(w_out[:, cb * 256:(cb + 1) * 256].rearrange("(k p) n -> p k n", p=128), 32, 256)
            for st in range(4):
                pb = P.rot('pj', ['pj0', 'pj1'])
                for kc in range(32):
                    mm(PS[pb][:, 0:256], mixedT[:, kc, st * 128:(st + 1) * 128], slot[:, kc, :], kc == 0, kc == 31,
                       [skey, ('mixT', kc // 4)], [pb])
                addB(st, cb * 256, 256, PS[pb][:, 0:256], pb)
        if stop != 'mix':
          tile_cross()
        if stop not in ('mix', 'cross'):
          tile_ffn()
        tile_final(oi)

    def tile_cross():
        fenceC()
        rmsnorm_T(4, lambda st: Bx[:, st, :], lambda st: bkeys(st), C_GCR)
        fenceC()
        qc = Cc[:, 0:4096].rearrange("p (a b) -> p a b", a=8)
        ocT = Cc[:, 4096:8192].rearrange("p (a b) -> p a b", a=8)
        KTh = Cc[:, 8192:10240].rearrange("p (a b) -> p a b", a=8)
        Vh = Cc[:, 10240:12288].rearrange("p (a b) -> p a b", a=2)
        eT = Cc[:, 12288:13312].rearrange("p (a b) -> p a b", a=2)
        rden = Cc[:, 13312:14336].bitcast(F32)
        for h in range(4):
            def ev_q(m, ps, pb):
                cp(qc[:, m, :], ps, [pb], ['qc'], eng='act' if m % 2 else 'dve')
            proj_fm(wq, h * 1024, 8, ev_q)
            dma('sp', KTh, KT_d[h * 8:(h + 1) * 8].rearrange("k p m -> p k m"), ['KT_d'], ['KTh'])
            for mc in range(2):
                dma('sp', Vh[:, mc, :], V_d[mc][:, h * 1024:(h + 1) * 1024], ['V_d'], ['Vh'])
            for mc in range(2):
                pb = ['mS', 'mO'][mc]
                for dc in range(8):
                    mm(PS[pb][:, :], KTh[:, dc, mc * 128:(mc + 1) * 128], qc[:, dc, :], dc == 0, dc == 7,
                       ['KTh', 'qc'], [pb])
                act(eT[:, mc, :], PS[pb][:, :], AF.Exp, [pb], ['eT'], scale=1.0 / 32.0)
            for mc in range(2):
                mm(PS['mA'][:, :], ONESB[:, :], eT[:, mc, :], mc == 0, mc == 1, ['cst', 'eT'], ['mA'])
            P.op('dve', lambda e, a=rden: e.reciprocal(out=a, in_=PS['mA'][:, :]), ['mA'], ['rden'])
            for dc in range(8):
                pb = P.rot('pj', ['pj0', 'pj1'])
                for mc in range(2):
                    mm(PS[pb][:, :], Vh[:, mc, dc * 128:(dc + 1) * 128], eT[:, mc, :], mc == 0, mc == 1,
                       ['Vh', 'eT'], [pb])
                tt(ocT[:, dc, :], PS[pb][:, :], rden, ALU.mult, [pb, 'rden'], ['ocT'])
            for cb in range(4):
                slot, skey = load_w(wo[h * 1024:(h + 1) * 1024, cb * 1024:(cb + 1) * 1024].rearrange("(k p) n -> p k n", p=128), 8, 1024)
                for st in range(4):
                    for hf in range(2):
                        pb = P.rot('pj', ['pj0', 'pj1'])
                        for kc in range(8):
                            mm(PS[pb][:, :], ocT[:, kc, st * 128:(st + 1) * 128], slot[:, kc, hf * 512:(hf + 1) * 512],
                               kc == 0, kc == 7, [skey, 'ocT'], [pb])
                        addB(st, cb * 1024 + hf * 512, 512, PS[pb][:, :], pb)
    def tile_ffn():
        fenceC()
        rmsnorm_T(4, lambda st: Bx[:, st, :], lambda st: bkeys(st), C_GFFN)
        fenceC()
        hidT = Cc[:, 0:11264].rearrange("p (a b) -> p a b", a=22)
        sg = [Cc[:, 11264 + i * 1024:11264 + (i + 1) * 1024].bitcast(F32) for i in range(2)]
        for (hc0, nch) in HID_BLOCKS:
            ci = 0
            while ci < nch:
                nm = min(2, nch - ci)
                col = (hc0 + ci) * 128
                sg_, kg = load_w(w_gate[:, col:col + nm * 128].rearrange("(k p) n -> p k n", p=128), 32, nm * 128)
                su_, ku = load_w(w_up[:, col:col + nm * 128].rearrange("(k p) n -> p k n", p=128), 32, nm * 128)
                for mi in range(nm):
                    i = (ci + mi) % 2
                    bg, bu = [('mS', 'mO'), ('mA', 'mU')][i]
                    for kc in range(32):
                        mm(PS[bg][:, :], sg_[:, kc, mi * 128:(mi + 1) * 128], nT[:, kc, :], kc == 0, kc == 31,
                           [kg] + NTK, [bg])
                    for kc in range(32):
                        mm(PS[bu][:, :], su_[:, kc, mi * 128:(mi + 1) * 128], nT[:, kc, :], kc == 0, kc == 31,
                           [ku] + NTK, [bu])
                    act(sg[i], PS[bg][:, :], AF.Silu, [bg], [('sg', i)])
                    tt(hidT[:, ci + mi, :], sg[i], PS[bu][:, :], ALU.mult, [('sg', i), bu], ['hidT'])
                ci += nm
            for cb in range(16):
                slot, skey = load_w(w_down[hc0 * 128:(hc0 + nch) * 128, cb * 256:(cb + 1) * 256].rearrange("(k p) n -> p k n", p=128), nch, 256)
                for st in range(4):
                    pb = P.rot('pj', ['pj0', 'pj1'])
                    for kc in range(nch):
                        mm(PS[pb][:, 0:256], hidT[:, kc, st * 128:(st + 1) * 128], slot[:, kc, :], kc == 0, kc == nch - 1,
                           [skey, 'hidT'], [pb])
                    addB(st, cb * 256, 256, PS[pb][:, 0:256], pb)
    def tile_final(oi):
        fenceC()
        dma('sp', gfin_sb, gfin_d[:, :], [], ['gfin'])
        for st in range(4):
            ssv = SM[:, 2 * st:2 * st + 1]
            rsv = SM[:, 2 * st + 1:2 * st + 2]
            ms(ssv, 0.0, [('ss', st)])
            act(junk, Bx[:, st, :], AF.Square, bkeys(st), [('ss', st)], accum=ssv)
            ts(rsv, ssv, 1.0 / D, EPS, ALU.mult, ALU.add, [('ss', st)], [('rs', st)])
            act(rsv, rsv, AF.Ln, [('rs', st)], [('rs', st)])
            act(rsv, rsv, AF.Exp, [('rs', st)], [('rs', st)], scale=-0.5)
            stt(Bx[:, st, :], Bx[:, st, :], rsv, gfin_sb, ALU.mult, ALU.mult, bkeys(st) + [('rs', st), 'gfin'], bkeys(st))
            dma('sp', out[oi * T + st * 128:oi * T + (st + 1) * 128, :], Bx[:, st, :], bkeys(st), [('out', oi, st)])

    def tile_state(ti, want_q_halo):
        fenceC()
        load_x(ti)
        rmsnorm_T(4, lambda st: Bx[:, st, :], lambda st: bkeys(st), C_GMIX)
        fenceB()
        gates_tile(False)
        for h in range(4):
            gla_head(h, ti, False)
        for h in range(4):
            mlstm_head(h, ti, False, want_q_halo, ti)
        fenceB()

    TEMP_KEYS = ['kT_f', 'qT_f', 'xk', 'xq', 'sp', 'elog', 'E1', 'E2', 'ktl', 'qtl', 'khT', 'kh_bf', 'AT', 'og',
                 'S_f', 'S_bf', 'Lf', 'DT', 'EB', 'n_f', 'n_bf', 'ssq', 'rsq', 'dn', 'gi', 'spf', 'colb', 'colbD',
                 'ef', 'wexp', 'nflag', 'k_bf', 'q_bf'] + [('v_bf', s) for s in range(4)] + [('gs', s) for s in range(4)]
    ALLB = [k for st in range(4) for k in bkeys(st)]

    CKEYS = [('mixT', i) for i in range(8)] + ['qc', 'ocT', 'KTh', 'Vh', 'eT', 'rden', 'hidT', ('sg', 0), ('sg', 1),
             'gfin', ('xn', 0), ('xn', 1), ('ktb', 0), ('ktb', 1), ('vtb', 0), ('vtb', 1)]

    def fenceB():
        P.op('dve', lambda e: e.memset(SM[:, 100:101], 0.0), [], ALLB + TEMP_KEYS)

    def fenceC():
        P.op('dve', lambda e: e.memset(SM[:, 101:102], 0.0), [], CKEYS)

    phase_mem()
    for ti in range(NT):
        if ti < npre:
            tile_state(ti, last_pre_q and ti == npre - 1)
        else:
            tile_full(ti, ti - npre)
    P.emit()
    nc._prog_stats = P.stats
    return nc


def _pack_consts(inp, flags):
    c = np.zeros((128, NCST), np.float32)
    c[:, C_ID:C_ID + 128] = np.eye(128, dtype=np.float32)
    tri = np.triu(np.ones((128, 128), np.float32))
    c[:, C_UG:C_UG + 128] = tri * (-1.0 / 16.0)
    c[:, C_UN:C_UN + 128] = -tri
    c[:, C_TRI:C_TRI + 128] = tri
    c[:, C_ONE:C_ONE + 128] = 1.0

    def fm(v):
        return np.ascontiguousarray(np.asarray(v, np.float32).reshape(32, 128).T)
    c[:, C_GMIX:C_GMIX + 32] = fm(inp["norm_mix_g"][0])
    c[:, C_GCR:C_GCR + 32] = fm(inp["norm_cross_g"][0])
    c[:, C_GFFN:C_GFFN + 32] = fm(inp["norm_ffn_g"][0])
    c[:, C_GMEM:C_GMEM + 32] = fm(inp["norm_mem_g"][0])
    gn = np.concatenate([np.asarray(inp["gla_norm_g"][0]).reshape(-1), np.asarray(inp["mlstm_norm_g"][0]).reshape(-1)])
    c[:, C_GNORM:C_GNORM + 32] = fm(gn)
    cw = np.asarray(inp["mlstm_conv_w"][0], np.float32)
    c[:, C_CW:C_CW + 64] = cw.T.reshape(16, 128, 4).transpose(1, 0, 2).reshape(128, 64)
    c[:, C_CB:C_CB + 16] = np.asarray(inp["mlstm_conv_b"][0], np.float32).reshape(16, 128).T
    c[:, C_BIF:C_BIF + 4] = np.asarray(inp["mlstm_igate_b"][0], np.float32)[None, :]
    c[:, C_BIF + 4:C_BIF + 8] = np.asarray(inp["mlstm_fgate_b"][0], np.float32)[None, :]
    c[:, C_FLAG:C_FLAG + len(flags)] = np.asarray(flags, np.float32)[None, :]
    return c


def _shared_maps(inp):
    w2 = np.zeros((32, 1024), np.float32)
    w2[0:16] = np.asarray(inp["gla_gate_w2"][0], np.float32)
    w2[16] = np.asarray(inp["gla_gate_b"][0], np.float32)
    m = {
        "w_in": np.ascontiguousarray(inp["w_in"][0]), "w_out": np.ascontiguousarray(inp["w_out"][0]),
        "wq": np.ascontiguousarray(inp["wq_c"][0]), "wk": np.ascontiguousarray(inp["wk_c"][0]),
        "wv": np.ascontiguousarray(inp["wv_c"][0]), "wo": np.ascontiguousarray(inp["wo_c"][0]),
        "w_gate": np.ascontiguousarray(inp["w_gate"][0]), "w_up": np.ascontiguousarray(inp["w_up"][0]),
        "w_down": np.ascontiguousarray(inp["w_down"][0]),
        "w2aug": w2,
        "gfin": np.ascontiguousarray(np.broadcast_to(np.asarray(inp["norm_final_g"], np.float32)[None, :], (128, D))),
    }
    return m


NPRE_FULL = 12
NOWN_FULL = 4


def kernel(**inputs):
    inp = {k: np.asarray(v) for k, v in inputs.items()}
    x = inp["x"]
    mem = inp["mem"]
    nc = build(NPRE_FULL, NOWN_FULL)
    shared = _shared_maps(inp)
    in_maps = []
    for c in range(8):
        b, j = c // 4, c % 4
        nz = (NPRE_FULL - 4 * j) * T
        xs = np.concatenate([np.zeros((nz, D), np.float32), x[b, 0:(4 * j + 4) * T]], axis=0)
        flags = [1.0 if i >= NPRE_FULL - 4 * j else 0.0 for i in range(NPRE_FULL)]
        m = dict(shared)
        m["xs"] = np.ascontiguousarray(xs)
        m["mem"] = np.ascontiguousarray(mem[b])
        m["cst"] = _pack_consts(inp, flags)
        in_maps.append(m)
    res = run_bass_kernel_spmd(nc, in_maps, core_ids=list(range(8)))
    outp = np.zeros((2, 8192, D), np.float32)
    for c in range(8):
        b, j = c // 4, c % 4
        outp[b, j * 2048:(j + 1) * 2048] = res.results[c]["out"]
    return outp
```

```python
import numpy as np
import concourse.bass as bass
import concourse.mybir as mybir
from concourse.bass_utils import run_bass_kernel_spmd

F32 = mybir.dt.float32
BF16 = mybir.dt.bfloat16
AF = mybir.ActivationFunctionType
ALU = mybir.AluOpType

D = 4096
T = 512
EPS = 1e-6
NSLOT = 4
SLOT = 8192
FH = 11008
HID_BLOCKS = [(0, 22), (22, 22), (44, 21), (65, 21)]

C_ID, C_UG, C_UN, C_TRI, C_ONE = 0, 128, 256, 384, 512
C_GMIX, C_GCR, C_GFFN, C_GMEM, C_GNORM = 640, 672, 704, 736, 768
C_CW, C_CB, C_BIF, C_FLAG = 800, 864, 880, 888
NCST = 904
LN16 = float(np.log(1.0 / 16.0))


class Prog:
    EPOCH = 30000
    KDMA = 8

    def __init__(self, nc):
        self.nc = nc
        self.eng = {'pe': nc.tensor, 'act': nc.scalar, 'dve': nc.vector, 'pool': nc.gpsimd, 'sp': nc.sync}
        self.ops = []
        self.last_w = {}
        self.readers = {}
        self.rotc = {}
        self.cap = None

    def begin(self):
        self.cap = []

    def mark(self):
        if self.cap is not None:
            self.cap.append(None)

    def end(self):
        c, self.cap = self.cap, None
        return c

    def commit(self, lst):
        for t in lst:
            if t is not None:
                self.op(*t)

    @staticmethod
    def merge(a, b):
        def segs(l):
            out, cur = [], []
            for t in l:
                if t is None:
                    if cur:
                        out.append(cur)
                    cur = []
                else:
                    cur.append(t)
            if cur:
                out.append(cur)
            return out
        sa, sb = segs(a), segs(b)
        res = []
        i = j = 0
        na, nb = len(sa), len(sb)
        while i < na or j < nb:
            if j < nb and (i >= na or j * na <= i * nb):
                res.extend(sb[j])
                j += 1
            else:
                res.extend(sa[i])
                i += 1
        return res

    def op(self, eng, fn, r=(), w=(), dma=False):
        if self.cap is not None:
            self.cap.append((eng, fn, list(r), list(w), dma))
            return None
        i = len(self.ops)
        deps = set()
        for k in r:
            lw = self.last_w.get(k)
            if lw is not None:
                deps.add(lw)
        for k in w:
            lw = self.last_w.get(k)
            if lw is not None:
                deps.add(lw)
            rd = self.readers.get(k)
            if rd:
                deps.update(rd[0].values())
                deps.update(rd[1])
        for k in r:
            rd = self.readers.get(k)
            if rd is None:
                rd = self.readers[k] = ({}, [])
            if dma:
                rd[1].append(i)
            else:
                rd[0][eng] = i
        for k in w:
            self.last_w[k] = i
            self.readers[k] = ({}, [])
        deps.discard(i)
        latest = {}
        fd = []
        for d in deps:
            o = self.ops[d]
            if o['dma']:
                fd.append(d)
                continue
            if o['eng'] == 'pe' and eng == 'pe':
                continue
            if o['eng'] not in latest or latest[o['eng']] < d:
                latest[o['eng']] = d
        fd.extend(latest.values())
        self.ops.append({'eng': eng, 'fn': fn, 'deps': fd, 'dma': dma, 'inc': False})
        return i

    def rot(self, grp, names):
        c = self.rotc.get(grp, 0)
        self.rotc[grp] = c + 1
        return names[c % len(names)]

    def emit(self):
        nc = self.nc
        ops = self.ops
        for o in ops:
            for d in o['deps']:
                ops[d]['inc'] = True
        cnt = {e: 0 for e in self.eng}
        dcnt = {e: 0 for e in self.eng}
        sems = {}

        def getsem(name):
            if name not in sems:
                sems[name] = nc.alloc_semaphore(name)
            return sems[name]
        comp = [None] * len(ops)
        pre = [None] * len(ops)
        for i, o in enumerate(ops):
            e = o['eng']
            if o['dma']:
                j = dcnt[e]
                dcnt[e] += 1
                nm = "d_%s_%d" % (e, j % self.KDMA)
                comp[i] = (nm, 16 * (j // self.KDMA + 1))
                if j >= self.KDMA:
                    pre[i] = (nm, 16 * (j // self.KDMA))
            elif o['inc']:
                c = cnt[e]
                cnt[e] += 1
                comp[i] = ("c_%s_%d" % (e, c // self.EPOCH), c % self.EPOCH + 1)
        known = {e: {} for e in self.eng}
        final = {}
        nwait = 0
        for i, o in enumerate(ops):
            e = o['eng']
            h = self.eng[e]
            waits = [comp[d] for d in o['deps']]
            if pre[i] is not None:
                waits.append(pre[i])
            kn = known[e]
            best = {}
            for (nm, v) in waits:
                if kn.get(nm, 0) >= v:
                    continue
                if nm not in best or best[nm] < v:
                    best[nm] = v
            for nm, v in best.items():
                h.wait_ge(getsem(nm), v)
                kn[nm] = v
                nwait += 1
            ins = o['fn'](h)
            if comp[i] is not None:
                nm, v = comp[i]
                ins.then_inc(getsem(nm), 16 if o['dma'] else 1)
                if o['dma']:
                    final[nm] = v
        for nm, v in final.items():
            nc.sync.wait_ge(getsem(nm), v)
        self.stats = {'ops': len(ops), 'waits': nwait, 'cnt': cnt, 'dcnt': dcnt, 'sems': len(sems)}


def build(npre, nown, last_pre_q=True, stop=None, dbg=False):
    nc = bass.Bass("TRN2", target_bir_lowering=False)
    P = Prog(nc)
    NT = npre + nown

    def din(name, shape, dt=F32):
        return nc.dram_tensor(name, shape, dt, kind="ExternalInput").ap()
    xs = din("xs", [NT * T, D])
    mem = din("mem", [256, D])
    w_in = din("w_in", [D, 12312])
    w_out = din("w_out", [D, D])
    wq = din("wq", [D, D])
    wk = din("wk", [D, D])
    wv = din("wv", [D, D])
    wo = din("wo", [D, D])
    w_gate = din("w_gate", [D, FH])
    w_up = din("w_up", [D, FH])
    w_down = din("w_down", [FH, D])
    cst_d = din("cst", [128, NCST])
    w2_d = din("w2aug", [32, 1024])
    gfin_d = din("gfin", [128, D])
    out = nc.dram_tensor("out", [nown * T, D], F32, kind="ExternalOutput").ap()

    def dscr(name, shape, dt):
        return nc.dram_tensor(name, shape, dt, kind="Internal").ap()
    st_gla = dscr("st_gla", [4, 128, 1024], F32)
    st_mC = dscr("st_mC", [4, 128, 1024], F32)
    st_mn = dscr("st_mn", [4, 128, 2], F32)
    KT_d = dscr("KT_d", [32, 128, 256], BF16)
    V_d = dscr("V_d", [2, 128, D], BF16)

    A = nc.alloc_sbuf_tensor("A", [128, 16384], BF16)
    B = nc.alloc_sbuf_tensor("B", [128, 16384], F32)
    Cc = nc.alloc_sbuf_tensor("C", [128, 16384], BF16)
    Wt = [nc.alloc_sbuf_tensor("W%d" % i, [128, SLOT], BF16) for i in range(NSLOT)]
    CST = nc.alloc_sbuf_tensor("CST", [128, NCST], F32)
    W2 = nc.alloc_sbuf_tensor("W2AUG", [32, 1024], F32)
    WAG = nc.alloc_sbuf_tensor("WAG", [128, 512], BF16)
    WIF = nc.alloc_sbuf_tensor("WIF", [128, 256], BF16)
    HAL = nc.alloc_sbuf_tensor("HAL", [128, 48], F32)
    SM = nc.alloc_sbuf_tensor("SM", [128, 128], F32)
    ONESB = nc.alloc_sbuf_tensor("ONESB", [128, 128], BF16)
    AGT = nc.alloc_sbuf_tensor("AGT", [32, 512], F32)
    PS = {n: nc.alloc_psum_tensor(n, [128, 512], F32) for n in
          ['pj0', 'pj1', 'pt0', 'pt1', 'mA', 'mS', 'mO', 'mU']}

    nT = A[:, :].rearrange("p (k t) -> p k t", k=32)
    Bx = B[:, :].rearrange("p (s c) -> p s c", s=4)
    ident = CST[:, C_ID:C_ID + 128]
    Ugla = CST[:, C_UG:C_UG + 128]
    Uneg = CST[:, C_UN:C_UN + 128]
    tri = CST[:, C_TRI:C_TRI + 128]
    onesf = CST[:, C_ONE:C_ONE + 128]

    def mm(out_, lhsT, rhs, start, stop, r, w):
        return P.op('pe', lambda e, a=out_, b=lhsT, c=rhs, s=start, t=stop: e.matmul(a, lhsT=b, rhs=c, start=s, stop=t), r, w)

    def tr(out_, in_, r, w):
        return P.op('pe', lambda e, a=out_, b=in_: e.transpose(a, b, ident), list(r) + ['cst'], w)

    def act(out_, in_, func, r, w, bias=None, scale=None, accum=None):
        kw = {}
        if bias is not None:
            kw['bias'] = bias
        if scale is not None:
            kw['scale'] = scale
        if accum is not None:
            kw['accum_out'] = accum
        return P.op('act', lambda e, a=out_, b=in_, f=func, k=kw: e.activation(out=a, in_=b, func=f, **k), r, w)

    def tt(out_, in0, in1, op_, r, w, eng='dve'):
        return P.op(eng, lambda e, a=out_, b=in0, c=in1, o=op_: e.tensor_tensor(out=a, in0=b, in1=c, op=o), r, w)

    def ts(out_, in0, s1, s2, op0, op1, r, w, eng='dve'):
        if s2 is None:
            return P.op(eng, lambda e, a=out_, b=in0, c=s1, o=op0: e.tensor_single_scalar(out=a, in_=b, scalar=c, op=o), r, w)
        return P.op(eng, lambda e, a=out_, b=in0, c=s1, d=s2, o=op0, q=op1: e.tensor_scalar(out=a, in0=b, scalar1=c, scalar2=d, op0=o, op1=q), r, w)

    def stt(out_, in0, sc, in1, op0, op1, r, w, eng='dve'):
        return P.op(eng, lambda e, a=out_, b=in0, c=sc, d=in1, o=op0, q=op1: e.scalar_tensor_tensor(out=a, in0=b, scalar=c, in1=d, op0=o, op1=q), r, w)

    def cp(out_, in_, r, w, eng='dve'):
        if eng == 'act':
            return P.op(eng, lambda e, a=out_, b=in_: e.activation(out=a, in_=b, func=AF.Copy), r, w)
        return P.op(eng, lambda e, a=out_, b=in_: e.tensor_copy(out=a, in_=b), r, w)

    def ms(ap, val, w, eng='dve'):
        return P.op(eng, lambda e, a=ap, v=val: e.memset(a, v), [], w)

    def dma(q, out_, in_, r, w):
        return P.op(q, lambda e, a=out_, b=in_: e.dma_start(out=a, in_=b), r, w, dma=True)

    def dump(name, ap, keys):
        shp = list(ap.shape)
        d = nc.dram_tensor("dbg_" + name, shp, F32, kind="ExternalOutput").ap()
        dma('pool', d, ap, keys, [('dbg', name)])

    def load_w(src, a, b):
        c = P.rotc.get('wslot', 0)
        P.rotc['wslot'] = c + 1
        si = c % NSLOT
        view = Wt[si][:, 0:a * b].rearrange("p (a b) -> p a b", a=a)
        key = ('W', si)
        dma('pool', view, src, [], [key])
        return view, key

    def bc3(ap2, n):
        return ap2.unsqueeze(2).broadcast_to([128, ap2.shape[1], n])

    dma('sp', CST[:, :], cst_d[:, :], [], ['cst'])
    dma('sp', W2[:, :], w2_d[:, :], [], ['w2'])
    dma('pool', WAG[:, :].rearrange("p (k n) -> p k n", k=32),
        w_in[:, 6144:6160].rearrange("(k p) n -> p k n", p=128), [], ['wag'])
    dma('pool', WIF[:, :].rearrange("p (k n) -> p k n", k=32),
        w_in[:, 12304:12312].rearrange("(k p) n -> p k n", p=128), [], ['wif'])
    ms(HAL[:, :], 0.0, ['hal'])
    ms(AGT[:, :], 1.0, ['agt'])
    ms(ONESB[:, :], 1.0, ['cst'])
    WAG3 = WAG[:, :].rearrange("p (k n) -> p k n", k=32)
    WIF3 = WIF[:, :].rearrange("p (k n) -> p k n", k=32)

    junk = Cc[:, 0:4096]
    gfin_sb = Cc[:, 4096:12288].bitcast(F32)
    XN = [Cc[:, 12288 + i * 1024:12288 + (i + 1) * 1024].bitcast(F32) for i in range(2)]
    mixedT = Cc[:, :].rearrange("p (k t) -> p k t", k=32)

    def rmsnorm_T(nst, src, skeys, gcol, ncols_tok=512):
        for st in range(nst):
            ssv = SM[:, 2 * st:2 * st + 1]
            rsv = SM[:, 2 * st + 1:2 * st + 2]
            ms(ssv, 0.0, [('ss', st)])
            act(junk, src(st), AF.Square, skeys(st), [('ss', st)], accum=ssv)
            ts(rsv, ssv, 1.0 / D, EPS, ALU.mult, ALU.add, [('ss', st)], [('rs', st)])
            act(rsv, rsv, AF.Ln, [('rs', st)], [('rs', st)])
            act(rsv, rsv, AF.Exp, [('rs', st)], [('rs', st)], scale=-0.5)
            for cb in range(8):
                xn = XN[cb % 2]
                act(xn, src(st)[:, cb * 512:(cb + 1) * 512], AF.Copy, list(skeys(st)) + [('rs', st)], [('xn', cb % 2)], scale=rsv)
                pb = P.rot('pt', ['pt0', 'pt1'])
                for j in range(4):
                    tr(PS[pb][:, j * 128:(j + 1) * 128], xn[:, j * 128:(j + 1) * 128], [('xn', cb % 2)], [pb])
                tt(nT[:, cb * 4:(cb + 1) * 4, st * 128:(st + 1) * 128],
                   PS[pb][:, :].rearrange("p (a b) -> p a b", a=4),
                   bc3(CST[:, gcol + cb * 4:gcol + cb * 4 + 4], 128), ALU.mult,
                   [pb, 'cst'], [('nT', st)])

    def bkeys(st, c0=0, c1=16):
        return [('B', st, c) for c in range(c0, c1)]

    NTK = [('nT', s) for s in range(4)]

    def proj_fm(wd, c0, nch, evac, ntok=512, nk=None):
        nk = nk or NTK
        m = 0
        while m < nch:
            nm = min(2, nch - m)
            slot, skey = load_w(wd[:, c0 + m * 128:c0 + (m + nm) * 128].rearrange("(k p) n -> p k n", p=128), 32, nm * 128)
            for mi in range(nm):
                pb = P.rot('pj', ['pj0', 'pj1'])
                for kc in range(32):
                    mm(PS[pb][:, 0:ntok], slot[:, kc, mi * 128:(mi + 1) * 128], nT[:, kc, 0:ntok], kc == 0, kc == 31,
                       [skey] + nk, [pb])
                evac(m + mi, PS[pb][:, 0:ntok], pb)
            m += nm

    def proj_tm(wd, c0, npiece, evac, nst=4):
        for pc in range(npiece):
            slot, skey = load_w(wd[:, c0 + pc * 256:c0 + (pc + 1) * 256].rearrange("(k p) n -> p k n", p=128), 32, 256)
            for st in range(nst):
                pb = P.rot('pj', ['pj0', 'pj1'])
                for kc in range(32):
                    mm(PS[pb][:, 0:256], nT[:, kc, st * 128:(st + 1) * 128], slot[:, kc, :], kc == 0, kc == 31,
                       [skey, ('nT', st)], [pb])
                evac(pc, st, PS[pb][:, 0:256], pb)

    def phase_mem():
        for st in range(2):
            dma('sp', Bx[:, st, :], mem[st * 128:(st + 1) * 128, :], [], bkeys(st))
        rmsnorm_T(2, lambda st: Bx[:, st, :], lambda st: bkeys(st), C_GMEM)
        ktb = [Cc[:, 14336 + i * 256:14336 + (i + 1) * 256] for i in range(2)]

        def ev_k(m, ps, pb):
            i = m % 2
            cp(ktb[i], ps, [pb], [('ktb', i)], eng='act')
            dma('sp', KT_d[m], ktb[i], [('ktb', i)], ['KT_d'])
        proj_fm(wk, 0, 32, ev_k, ntok=256, nk=[('nT', 0), ('nT', 1)])
        vtb = [Cc[:, 14848 + i * 256:14848 + (i + 1) * 256] for i in range(2)]

        def ev_v(pc, st, ps, pb):
            i = (pc * 2 + st) % 2
            cp(vtb[i], ps, [pb], [('vtb', i)], eng='dve')
            dma('sp', V_d[st][:, pc * 256:(pc + 1) * 256], vtb[i], [('vtb', i)], ['V_d'])
        proj_tm(wv, 0, 16, ev_v, nst=2)

    def bsub(off, n):
        return B[:, off:off + n]
    qT_f = bsub(0, 1024).rearrange("p (a b) -> p a b", a=2)
    kT_f = bsub(1024, 1024).rearrange("p (a b) -> p a b", a=2)
    v_bf = bsub(2048, 1024).bitcast(BF16).rearrange("p (a b) -> p a b", a=4)
    gs_f = bsub(3072, 2048).rearrange("p (a b) -> p a b", a=4)
    sp_f = bsub(5120, 1024).rearrange("p (a b) -> p a b", a=4)
    qk_bf = bsub(5120, 1024).bitcast(BF16)
    q_bf = qk_bf[:, 0:1024].rearrange("p (a b) -> p a b", a=2)
    k_bf = qk_bf[:, 1024:2048].rearrange("p (a b) -> p a b", a=2)
    xq = bsub(6144, 1030).rearrange("p (a b) -> p a b", a=2)
    xk = bsub(7174, 1030).rearrange("p (a b) -> p a b", a=2)
    R0 = 8208
    E1a = bsub(R0, 1024).rearrange("p (a b) -> p a b", a=2)
    E2a = bsub(R0 + 1024, 1024).rearrange("p (a b) -> p a b", a=2)
    DTa = bsub(R0, 512).rearrange("p (a b) -> p a b", a=4)
    EBa = bsub(R0 + 512, 512).rearrange("p (a b) -> p a b", a=4)
    Lf2 = [bsub(R0 + 1024 + i * 128, 128) for i in range(2)]
    qtla = bsub(R0 + 2048, 512).bitcast(BF16).rearrange("p (a b) -> p a b", a=2)
    ktla = bsub(R0 + 2560, 512).bitcast(BF16).rearrange("p (a b) -> p a b", a=2)
    khT2 = [bsub(R0 + 3072 + i * 256, 256).rearrange("p (a b) -> p a b", a=2) for i in range(2)]
    kha = bsub(R0 + 3584, 512).bitcast(BF16).rearrange("p (a b) -> p a b", a=4)
    ATa = bsub(R0 + 4096, 256).bitcast(BF16).rearrange("p (a b) -> p a b", a=4)
    og2 = [bsub(R0 + 4352 + i * 512, 512) for i in range(2)]
    S_f = bsub(R0 + 5376, 1024).rearrange("p (a b) -> p a b", a=2)
    S_bf = bsub(R0 + 6400, 512).bitcast(BF16).rearrange("p (a b) -> p a b", a=2)
    elog2 = [bsub(R0 + 6912 + i * 256, 256) for i in range(2)]
    junk2 = bsub(R0 + 7424, 256).bitcast(BF16)
    SB0 = R0 + 7680
    n_f = bsub(SB0, 2)
    n_bf = bsub(SB0 + 2, 1).bitcast(BF16)
    sm2 = bsub(SB0 + 4, 32)
    gi_sb = bsub(SB0 + 36, 32).rearrange("p (a b) -> p a b", a=4)
    spf = bsub(SB0 + 68, 16).rearrange("p (a b) -> p a b", a=4)
    colb = bsub(SB0 + 84, 16).rearrange("p (a b) -> p a b", a=4)
    colbD = bsub(SB0 + 100, 16).rearrange("p (a b) -> p a b", a=4)
    ef_t = bsub(SB0 + 116, 4)
    wexpa = bsub(SB0 + 120, 4)
    nflag = bsub(SB0 + 124, 2)

    HBL = [(PS['mA'][:, 0:256], 'mA'), (PS['mS'][:, 0:256], 'mS'), (PS['mU'][:, 0:256], 'mU')]

    def hb():
        return P.rot('hb', HBL)

    def gnorm_bc(c4):
        return bc3(CST[:, C_GNORM + c4:C_GNORM + c4 + 4], 128)

    def out_epilogue(h8, st, ogi):
        pb = P.rot('pt', ['pt0', 'pt1'])
        for j in range(4):
            tr(PS[pb][:, j * 128:(j + 1) * 128], og2[ogi][:, j * 128:(j + 1) * 128], [('og', ogi)], [pb])
        tt(mixedT[:, h8 * 4:(h8 + 1) * 4, st * 128:(st + 1) * 128],
           PS[pb][:, :].rearrange("p (a b) -> p a b", a=4), gnorm_bc(h8 * 4), ALU.mult,
           [pb, 'cst'], [('mixT', h8)])

    def gates_tile(full):
        pb = P.rot('pj', ['pj0', 'pj1'])
        for kc in range(32):
            mm(PS[pb][0:16, :], WAG3[:, kc, :], nT[:, kc, :], kc == 0, kc == 31, ['wag'] + NTK, [pb])
        cp(AGT[0:16, :], PS[pb][0:16, :], [pb], ['agt'], eng='act')
        for st in range(4):
            pb = P.rot('pj', ['pj0', 'pj1'])
            for kc in range(32):
                mm(PS[pb][:, 0:8], nT[:, kc, st * 128:(st + 1) * 128], WIF3[:, kc, :], kc == 0, kc == 31,
                   ['wif', ('nT', st)], [pb])
            tt(gi_sb[:, st, :], PS[pb][:, 0:8], CST[:, C_BIF:C_BIF + 8], ALU.add, [pb, 'cst'], [('gi', st)])
            act(ef_t, gi_sb[:, st, 4:8], AF.Exp, [('gi', st)], ['ef'], scale=-1.0)
            act(spf[:, st, :], ef_t, AF.Ln, ['ef'], [('spf', st)], bias=1.0)
            hp, hk = hb()
            mm(hp[:, 0:4], Uneg, spf[:, st, :], True, True, ['cst', ('spf', st)], [hk])
            tt(colb[:, st, :], gi_sb[:, st, 0:4], hp[:, 0:4], ALU.subtract, [('gi', st), hk], [('colb', st)])
            ts(colbD[:, st, :], colb[:, st, :], LN16, None, ALU.add, None, [('colb', st)], [('colbD', st)])

    def gla_head(h, ti, full):
        first = (ti == 0)

        def ev_k(m, ps, pb):
            cp(kT_f[:, m, :], ps, [pb], ['kT_f'], eng='act')
        proj_fm(w_in, 1024 + h * 256, 2, ev_k)

        def ev_v(pc, st, ps, pb):
            cp(v_bf[:, st, pc * 256:(pc + 1) * 256], ps, [pb], [('v_bf', st)], eng='dve')
        proj_tm(w_in, 2048 + h * 512, 2, ev_v)
        if full:
            def ev_q(m, ps, pb):
                act(qT_f[:, m, :], ps, AF.Copy, [pb], ['qT_f'], scale=1.0 / 16.0)
            proj_fm(w_in, h * 256, 2, ev_q)

            def ev_g(pc, st, ps, pb):
                act(gs_f[:, st, pc * 256:(pc + 1) * 256], ps, AF.Silu, [pb], [('gs', st)])
            proj_tm(w_in, 4096 + h * 512, 2, ev_g)
        if first:
            ms(S_f[:, :, :], 0.0, ['S_f'])
        else:
            dma('sp', S_f[:, :, :], st_gla[h].rearrange("p (a b) -> p a b", a=2), [('st_gla', h)], ['S_f'])
        cp(S_bf[:, :, :], S_f[:, :, :], ['S_f'], ['S_bf'], eng='act')
        for st in range(4):
            hp, hk = hb()
            mm(hp, AGT[0:32, st * 128:(st + 1) * 128], W2[0:32, h * 256:(h + 1) * 256], True, True, ['agt', 'w2'], [hk])
            el = elog2[st % 2]
            act(el, hp, AF.Exp, [hk], [('elog', st % 2)], scale=-1.0)
            act(sp_f[:, st, :], el, AF.Ln, [('elog', st % 2)], [('sp', st)], bias=1.0)
        for st in range(4):
            sl = slice(st * 128, (st + 1) * 128)
            hp, hk = hb()
            for dc in range(2):
                mm(hp[:, dc * 128:(dc + 1) * 128], sp_f[:, st, dc * 128:(dc + 1) * 128], Ugla, True, True,
                   [('sp', st), 'cst'], [hk])
            pbT = hp.rearrange("p (a b) -> p a b", a=2)
            act(E1a[:, :, sl], pbT, AF.Exp, [hk], [('E1', st)])
            act(E2a[:, :, sl], pbT, AF.Exp, [hk], [('E2', st)], scale=-1.0)
        for st in range(4):
            sl = slice(st * 128, (st + 1) * 128)
            tt(ktla[:, :, sl], kT_f[:, :, sl], E2a[:, :, sl], ALU.mult, ['kT_f', ('E2', st)], [('ktl', st)])
            if full:
                tt(qtla[:, :, sl], qT_f[:, :, sl], E1a[:, :, sl], ALU.mult, ['qT_f', ('E1', st)], [('qtl', st)])
            kt = khT2[st % 2]
            for dc in range(2):
                stt(kt[:, dc, :], kT_f[:, dc, sl], E1a[:, dc, st * 128 + 127:st * 128 + 128], E2a[:, dc, sl],
                    ALU.mult, ALU.mult, ['kT_f', ('E1', st), ('E2', st)], [('khT', st % 2)])
            pb = P.rot('pt', ['pt0', 'pt1'])
            for dc in range(2):
                tr(PS[pb][:, dc * 128:(dc + 1) * 128], kt[:, dc, :], [('khT', st % 2)], [pb])
            cp(kha[:, st, :], PS[pb][:, 0:256], [pb], [('kh', st)], eng='act')
        if full:
            for st in range(4):
                sl = slice(st * 128, (st + 1) * 128)
                hp, hk = hb()
                for dc in range(2):
                    mm(hp[:, 0:128], ktla[:, dc, sl], qtla[:, dc, sl], dc == 0, dc == 1,
                       [('ktl', st), ('qtl', st)], [hk])
                tt(ATa[:, st, :], hp[:, 0:128], tri, ALU.mult, [hk, 'cst'], [('AT', st)])
        for st in range(4):
            sl = slice(st * 128, (st + 1) * 128)
            if full:
                ob = 'mO'
                mm(PS[ob][:, :], ATa[:, st, :], v_bf[:, st, :], True, False, [('AT', st), ('v_bf', st)], [ob])
                for dc in range(2):
                    mm(PS[ob][:, :], qtla[:, dc, sl], S_bf[:, dc, :], False, dc == 1, [('qtl', st), 'S_bf'], [ob])
            for dc in range(2):
                pb2 = ['pj0', 'pj1'][dc]
                mm(PS[pb2][:, :], kha[:, st, dc * 128:(dc + 1) * 128], v_bf[:, st, :], True, True,
                   [('kh', st), ('v_bf', st)], [pb2])
                stt(S_f[:, dc, :], S_f[:, dc, :], E1a[:, dc, st * 128 + 127:st * 128 + 128], PS[pb2][:, :],
                    ALU.mult, ALU.add, ['S_f', ('E1', st), pb2], ['S_f'])
            cp(S_bf[:, :, :], S_f[:, :, :], ['S_f'], ['S_bf'], eng='act')
            if full:
                ssq = sm2[:, st * 4:st * 4 + 1]
                rs = sm2[:, st * 4 + 1:st * 4 + 2]
                ogi = st % 2
                ms(ssq, 0.0, [('ssq', st)])
                act(junk2, PS[ob][:, :], AF.Square, [ob], [('ssq', st)], accum=ssq)
                ts(rs, ssq, 1.0 / 512, EPS, ALU.mult, ALU.add, [('ssq', st)], [('rsq', st)])
                act(rs, rs, AF.Ln, [('rsq', st)], [('rsq', st)])
                act(rs, rs, AF.Exp, [('rsq', st)], [('rsq', st)], scale=-0.5)
                stt(og2[ogi], PS[ob][:, :], rs, gs_f[:, st, :], ALU.mult, ALU.mult,
                    [ob, ('rsq', st), ('gs', st)], [('og', ogi)])
                out_epilogue(h, st, ogi)
        dma('sp', st_gla[h].rearrange("p (a b) -> p a b", a=2), S_f[:, :, :], ['S_f'], [('st_gla', h)])

    def conv_silu(xin, ch0, outf, key_in, key_out):
        for dc in range(2):
            ch = ch0 + dc
            wcol = C_CW + ch * 4
            ts(outf[:, dc, :], xin[:, dc, 0:512], CST[:, wcol:wcol + 1], CST[:, C_CB + ch:C_CB + ch + 1],
               ALU.mult, ALU.add, [key_in, 'cst'], [key_out])
            for j in range(1, 4):
                stt(outf[:, dc, :], xin[:, dc, j:j + 512], CST[:, wcol + j:wcol + j + 1], outf[:, dc, :],
                    ALU.mult, ALU.add, [key_in, 'cst', key_out], [key_out])
        act(outf[:, :, :], outf[:, :, :], AF.Silu, [key_out], [key_out])

    def halo_in(xin, ch0, key):
        cp(xin[:, :, 0:3], HAL[:, ch0 * 3:(ch0 + 2) * 3].rearrange("p (a b) -> p a b", a=2), ['hal'], [key], eng='dve')

    def halo_out(xin, ch0, key):
        cp(HAL[:, ch0 * 3:(ch0 + 2) * 3].rearrange("p (a b) -> p a b", a=2), xin[:, :, 512:515], [key], ['hal'], eng='dve')

    def mlstm_head(h, ti, full, want_q_halo, pslot):
        first = (ti == 0)
        km_f = kT_f
        qm_f = qT_f
        halo_in(xk, 8 + h * 2, 'xk')

        def ev_k(m, ps, pb):
            cp(xk[:, m, 3:515], ps, [pb], ['xk'], eng='act')
        proj_fm(w_in, 7184 + h * 256, 2, ev_k)
        conv_silu(xk, 8 + h * 2, km_f, 'xk', 'kT_f')
        halo_out(xk, 8 + h * 2, 'xk')
        if full:
            cp(k_bf[:, :, :], km_f[:, :, :], ['kT_f'], ['k_bf'], eng='dve')

        def ev_v(pc, st, ps, pb):
            cp(v_bf[:, st, pc * 256:(pc + 1) * 256], ps, [pb], [('v_bf', st)], eng='dve')
        proj_tm(w_in, 8208 + h * 512, 2, ev_v)
        if full or want_q_halo:
            halo_in(xq, h * 2, 'xq')

            def ev_q(m, ps, pb):
                cp(xq[:, m, 3:515], ps, [pb], ['xq'], eng='act')
            proj_fm(w_in, 6160 + h * 256, 2, ev_q)
            if full:
                conv_silu(xq, h * 2, qm_f, 'xq', 'qT_f')
                cp(q_bf[:, :, :], qm_f[:, :, :], ['qT_f'], ['q_bf'], eng='dve')
            halo_out(xq, h * 2, 'xq')
        if full:
            def ev_o(pc, st, ps, pb):
                act(gs_f[:, st, pc * 256:(pc + 1) * 256], ps, AF.Sigmoid, [pb], [('gs', st)])
            proj_tm(w_in, 10256 + h * 512, 2, ev_o)
        if first:
            ms(S_f[:, :, :], 0.0, ['S_f'])
            ms(n_f, 0.0, ['n_f'])
        else:
            dma('sp', S_f[:, :, :], st_mC[h].rearrange("p (a b) -> p a b", a=2), [('st_mC', h)], ['S_f'])
            dma('sp', n_f, st_mn[h], [('st_mn', h)], ['n_f'])
        cp(S_bf[:, :, :], S_f[:, :, :], ['S_f'], ['S_bf'], eng='act')
        cp(n_bf, n_f, ['n_f'], ['n_bf'], eng='dve')
        for st in range(4):
            sl = slice(st * 128, (st + 1) * 128)
            Lf = Lf2[st % 2]
            ts(Lf, onesf, spf[:, st, h:h + 1], None, ALU.mult, None, ['cst', ('spf', st)], [('Lf', st % 2)])
            hp, hk = hb()
            mm(hp[:, 0:128], Lf, Uneg, True, True, [('Lf', st % 2), 'cst'], [hk])
            act(EBa[:, st, :], hp[:, 0:128], AF.Exp, [hk], [('EB', st)])
            act(wexpa[:, st:st + 1], hp[:, 127:128], AF.Exp, [hk, ('colb', st)], [('wexp', st)], bias=colb[:, st, h:h + 1])
            if full:
                act(DTa[:, st, :], hp[:, 0:128], AF.Exp, [hk, ('colbD', st)], [('DT', st)], bias=colbD[:, st, h:h + 1])
                tt(DTa[:, st, :], DTa[:, st, :], tri, ALU.mult, [('DT', st), 'cst'], [('DT', st)])
                for dc in range(2):
                    stt(qtla[:, dc, sl], qm_f[:, dc, sl], 1.0 / 16.0, EBa[:, st, :], ALU.mult, ALU.mult,
                        ['qT_f', ('EB', st)], [('qtl', st)])
            pb = P.rot('pt', ['pt0', 'pt1'])
            for dc in range(2):
                tr(PS[pb][:, dc * 128:(dc + 1) * 128], km_f[:, dc, sl], ['kT_f'], [pb])
            ts(kha[:, st, :], PS[pb][:, 0:256], wexpa[:, st:st + 1], None, ALU.mult, None, [pb, ('wexp', st)], [('kh', st)])
        if full:
            for st in range(4):
                sl = slice(st * 128, (st + 1) * 128)
                hp, hk = hb()
                for dc in range(2):
                    mm(hp[:, 0:128], k_bf[:, dc, sl], q_bf[:, dc, sl], dc == 0, dc == 1, ['k_bf', 'q_bf'], [hk])
                tt(ATa[:, st, :], hp[:, 0:128], DTa[:, st, :], ALU.mult, [hk, ('DT', st)], [('AT', st)])
        for st in range(4):
            sl = slice(st * 128, (st + 1) * 128)
            if full:
                ob = 'mO'
                mm(PS[ob][:, :], ATa[:, st, :], v_bf[:, st, :], True, False, [('AT', st), ('v_bf', st)], [ob])
                for dc in range(2):
                    mm(PS[ob][:, :], qtla[:, dc, sl], S_bf[:, dc, :], False, dc == 1, [('qtl', st), 'S_bf'], [ob])
                dp, dk_ = hb()
                mm(dp[:, 0:1], ATa[:, st, :], ONESB[:, 0:1], True, False, [('AT', st), 'cst'], [dk_])
                for dc in range(2):
                    mm(dp[:, 0:1], qtla[:, dc, sl], n_bf[:, dc:dc + 1], False, dc == 1, [('qtl', st), 'n_bf'], [dk_])
            np_, nk = hb()
            for dc in range(2):
                pb2 = ['pj0', 'pj1'][dc]
                mm(PS[pb2][:, :], kha[:, st, dc * 128:(dc + 1) * 128], v_bf[:, st, :], True, True,
                   [('kh', st), ('v_bf', st)], [pb2])
                stt(S_f[:, dc, :], S_f[:, dc, :], EBa[:, st, 127:128], PS[pb2][:, :], ALU.mult, ALU.add,
                    ['S_f', ('EB', st), pb2], ['S_f'])
                mm(np_[:, dc:dc + 1], kha[:, st, dc * 128:(dc + 1) * 128], ONESB[:, 0:1], True, True,
                   [('kh', st), 'cst'], [nk])
            if pslot is not None:
                ts(nflag, np_[:, 0:2], CST[:, C_FLAG + pslot:C_FLAG + pslot + 1], None, ALU.mult, None,
                   [nk, 'cst'], ['nflag'])
                stt(n_f, n_f, EBa[:, st, 127:128], nflag, ALU.mult, ALU.add, ['n_f', ('EB', st), 'nflag'], ['n_f'])
            else:
                stt(n_f, n_f, EBa[:, st, 127:128], np_[:, 0:2], ALU.mult, ALU.add, ['n_f', ('EB', st), nk], ['n_f'])
            cp(S_bf[:, :, :], S_f[:, :, :], ['S_f'], ['S_bf'], eng='act')
            cp(n_bf, n_f, ['n_f'], ['n_bf'], eng='dve')
            if full:
                ssq = sm2[:, st * 4:st * 4 + 1]
                rs = sm2[:, st * 4 + 1:st * 4 + 2]
                dn = sm2[:, st * 4 + 2:st * 4 + 3]
                ogi = st % 2
                act(dn, dp[:, 0:1], AF.Abs, [dk_], [('dn', st)])
                ts(dn, dn, 1.0, None, ALU.max, None, [('dn', st)], [('dn', st)])
                P.op('dve', lambda e, a=dn: e.reciprocal(out=a, in_=a), [('dn', st)], [('dn', st)])
                ms(ssq, 0.0, [('ssq', st)])
                act(junk2, PS[ob][:, :], AF.Square, [ob, ('dn', st)], [('ssq', st)], accum=ssq, scale=dn)
                ts(rs, ssq, 1.0 / 512, EPS, ALU.mult, ALU.add, [('ssq', st)], [('rsq', st)])
                act(rs, rs, AF.Ln, [('rsq', st)], [('rsq', st)])
                act(rs, rs, AF.Exp, [('rsq', st)], [('rsq', st)], scale=-0.5)
                tt(rs, rs, dn, ALU.mult, [('rsq', st), ('dn', st)], [('rsq', st)])
                stt(og2[ogi], PS[ob][:, :], rs, gs_f[:, st, :], ALU.mult, ALU.mult,
                    [ob, ('rsq', st), ('gs', st)], [('og', ogi)])
                out_epilogue(4 + h, st, ogi)
        dma('sp', st_mC[h].rearrange("p (a b) -> p a b", a=2), S_f[:, :, :], ['S_f'], [('st_mC', h)])
        dma('sp', st_mn[h], n_f, ['n_f'], [('st_mn', h)])


    def sset(s_):
        base = s_ * 4104
        return {
            'kT': bsub(base, 1024).rearrange("p (a b) -> p a b", a=2),
            'v': bsub(base + 1024, 1024).bitcast(BF16).rearrange("p (a b) -> p a b", a=4),
            'sp': bsub(base + 2048, 1024).rearrange("p (a b) -> p a b", a=4),
            'xk': bsub(base + 3072, 1030).rearrange("p (a b) -> p a b", a=2),
        }
    SSET = [sset(0), sset(1)]
    xq_st = bsub(R0 + 4096, 1030).rearrange("p (a b) -> p a b", a=2)
    HBS = [(PS['mA'][:, 0:256], 'mA'), (PS['mS'][:, 0:256], 'mS')]

    def hbs():
        return P.rot('hbs', HBS)

    def pfm(wd, c0, nch, evac):
        def ev(m, ps, pb):
            evac(m, ps, pb)
            P.mark()
        proj_fm(wd, c0, nch, ev)

    def ptm(wd, c0, npiece, evac):
        def ev(pc, st, ps, pb):
            evac(pc, st, ps, pb)
            P.mark()
        proj_tm(wd, c0, npiece, ev)

    def gla_sproj(h, s_):
        V = SSET[s_]

        def ev_k(m, ps, pb):
            cp(V['kT'][:, m, :], ps, [pb], [('skT', s_)], eng='act')
        pfm(w_in, 1024 + h * 256, 2, ev_k)

        def ev_v(pc, st, ps, pb):
            cp(V['v'][:, st, pc * 256:(pc + 1) * 256], ps, [pb], [('sv', s_, st)], eng='dve')
        ptm(w_in, 2048 + h * 512, 2, ev_v)

    def gla_srec(h, s_, ti):
        V = SSET[s_]
        kT, vb, sp = V['kT'], V['v'], V['sp']
        if ti == 0:
            ms(S_f[:, :, :], 0.0, ['S_f'])
        else:
            dma('sp', S_f[:, :, :], st_gla[h].rearrange("p (a b) -> p a b", a=2), [('st_gla', h)], ['S_f'])
        for st in range(4):
            hp, hk = hbs()
            mm(hp, AGT[0:32, st * 128:(st + 1) * 128], W2[0:32, h * 256:(h + 1) * 256], True, True, ['agt', 'w2'], [hk])
            el = elog2[st % 2]
            act(el, hp, AF.Exp, [hk], [('elog', st % 2)], scale=-1.0)
            act(sp[:, st, :], el, AF.Ln, [('elog', st % 2)], [('ssp', s_, st)], bias=1.0)
            P.mark()
        for st in range(4):
            sl = slice(st * 128, (st + 1) * 128)
            hp, hk = hbs()
            for dc in range(2):
                mm(hp[:, dc * 128:(dc + 1) * 128], sp[:, st, dc * 128:(dc + 1) * 128], Ugla, True, True,
                   [('ssp', s_, st), 'cst'], [hk])
            pbT = hp.rearrange("p (a b) -> p a b", a=2)
            act(E1a[:, :, sl], pbT, AF.Exp, [hk], [('E1', st)])
            act(E2a[:, :, sl], pbT, AF.Exp, [hk], [('E2', st)], scale=-1.0)
            P.mark()
        for st in range(4):
            sl = slice(st * 128, (st + 1) * 128)
            kt = khT2[st % 2]
            for dc in range(2):
                stt(kt[:, dc, :], kT[:, dc, sl], E1a[:, dc, st * 128 + 127:st * 128 + 128], E2a[:, dc, sl],
                    ALU.mult, ALU.mult, [('skT', s_), ('E1', st), ('E2', st)], [('khT', st % 2)])
            pb = P.rot('pt', ['pt0', 'pt1'])
            for dc in range(2):
                tr(PS[pb][:, dc * 128:(dc + 1) * 128], kt[:, dc, :], [('khT', st % 2)], [pb])
            cp(kha[:, st, :], PS[pb][:, 0:256], [pb], [('kh', st)], eng='act')
            P.mark()
        for st in range(4):
            for dc in range(2):
                pb2 = ['mO', 'mU'][dc]
                mm(PS[pb2][:, :], kha[:, st, dc * 128:(dc + 1) * 128], vb[:, st, :], True, True,
                   [('kh', st), ('sv', s_, st)], [pb2])
                stt(S_f[:, dc, :], S_f[:, dc, :], E1a[:, dc, st * 128 + 127:st * 128 + 128], PS[pb2][:, :],
                    ALU.mult, ALU.add, ['S_f', ('E1', st), pb2], ['S_f'])
            P.mark()
        dma('sp', st_gla[h].rearrange("p (a b) -> p a b", a=2), S_f[:, :, :], ['S_f'], [('st_gla', h)])

    def conv_silu_s(xin, ch0, outf, key_in, key_out):
        for dc in range(2):
            ch = ch0 + dc
            wcol = C_CW + ch * 4
            ts(outf[:, dc, :], xin[:, dc, 0:512], CST[:, wcol:wcol + 1], CST[:, C_CB + ch:C_CB + ch + 1],
               ALU.mult, ALU.add, [key_in, 'cst'], [key_out])
            for j in range(1, 4):
                stt(outf[:, dc, :], xin[:, dc, j:j + 512], CST[:, wcol + j:wcol + j + 1], outf[:, dc, :],
                    ALU.mult, ALU.add, [key_in, 'cst', key_out], [key_out])
        act(outf[:, :, :], outf[:, :, :], AF.Silu, [key_out], [key_out])

    def mlstm_sproj(h, s_, want_q_halo):
        V = SSET[s_]
        xk_ = V['xk']
        kx = ('sxk', s_)
        halo_in(xk_, 8 + h * 2, kx)

        def ev_k(m, ps, pb):
            cp(xk_[:, m, 3:515], ps, [pb], [kx], eng='act')
        pfm(w_in, 7184 + h * 256, 2, ev_k)
        conv_silu_s(xk_, 8 + h * 2, V['kT'], kx, ('skT', s_))
        halo_out(xk_, 8 + h * 2, kx)
        P.mark()

        def ev_v(pc, st, ps, pb):
            cp(V['v'][:, st, pc * 256:(pc + 1) * 256], ps, [pb], [('sv', s_, st)], eng='dve')
        ptm(w_in, 8208 + h * 512, 2, ev_v)
        if want_q_halo:
            halo_in(xq_st, h * 2, 'xq_st')

            def ev_q(m, ps, pb):
                cp(xq_st[:, m, 3:515], ps, [pb], ['xq_st'], eng='act')
            pfm(w_in, 6160 + h * 256, 2, ev_q)
            halo_out(xq_st, h * 2, 'xq_st')
            P.mark()

    def mlstm_srec(h, s_, ti, pslot):
        V = SSET[s_]
        km, vb = V['kT'], V['v']
        if ti == 0:
            ms(S_f[:, :, :], 0.0, ['S_f'])
            ms(n_f, 0.0, ['n_f'])
        else:
            dma('sp', S_f[:, :, :], st_mC[h].rearrange("p (a b) -> p a b", a=2), [('st_mC', h)], ['S_f'])
            dma('sp', n_f, st_mn[h], [('st_mn', h)], ['n_f'])
        for st in range(4):
            sl = slice(st * 128, (st + 1) * 128)
            Lf = Lf2[st % 2]
            ts(Lf, onesf, spf[:, st, h:h + 1], None, ALU.mult, None, ['cst', ('spf', st)], [('Lf', st % 2)])
            hp, hk = hbs()
            mm(hp[:, 0:128], Lf, Uneg, True, True, [('Lf', st % 2), 'cst'], [hk])
            act(EBa[:, st, :], hp[:, 0:128], AF.Exp, [hk], [('EB', st)])
            act(wexpa[:, st:st + 1], hp[:, 127:128], AF.Exp, [hk, ('colb', st)], [('wexp', st)], bias=colb[:, st, h:h + 1])
            P.mark()
            pb = P.rot('pt', ['pt0', 'pt1'])
            for dc in range(2):
                tr(PS[pb][:, dc * 128:(dc + 1) * 128], km[:, dc, sl], [('skT', s_)], [pb])
            ts(kha[:, st, :], PS[pb][:, 0:256], wexpa[:, st:st + 1], None, ALU.mult, None, [pb, ('wexp', st)], [('kh', st)])
            P.mark()
        for st in range(4):
            np_, nk = hbs()
            for dc in range(2):
                pb2 = ['mO', 'mU'][dc]
                mm(PS[pb2][:, :], kha[:, st, dc * 128:(dc + 1) * 128], vb[:, st, :], True, True,
                   [('kh', st), ('sv', s_, st)], [pb2])
                stt(S_f[:, dc, :], S_f[:, dc, :], EBa[:, st, 127:128], PS[pb2][:, :], ALU.mult, ALU.add,
                    ['S_f', ('EB', st), pb2], ['S_f'])
                mm(np_[:, dc:dc + 1], kha[:, st, dc * 128:(dc + 1) * 128], ONESB[:, 0:1], True, True,
                   [('kh', st), 'cst'], [nk])
            ts(nflag, np_[:, 0:2], CST[:, C_FLAG + pslot:C_FLAG + pslot + 1], None, ALU.mult, None,
               [nk, 'cst'], ['nflag'])
            stt(n_f, n_f, EBa[:, st, 127:128], nflag, ALU.mult, ALU.add, ['n_f', ('EB', st), 'nflag'], ['n_f'])
            P.mark()
        dma('sp', st_mC[h].rearrange("p (a b) -> p a b", a=2), S_f[:, :, :], ['S_f'], [('st_mC', h)])
        dma('sp', st_mn[h], n_f, ['n_f'], [('st_mn', h)])

    def state_mixers(ti, want_q_halo):
        def proj(i):
            P.begin()
            if i < 4:
                gla_sproj(i, i % 2)
            else:
                mlstm_sproj(i - 4, i % 2, want_q_halo)
            return P.end()

        def rec(i):
            P.begin()
            if i < 4:
                gla_srec(i, i % 2, ti)
            else:
                mlstm_srec(i - 4, i % 2, ti, ti)
            return P.end()
        P.commit(proj(0))
        for i in range(8):
            L1 = rec(i)
            L2 = proj(i + 1) if i < 7 else []
            P.commit(Prog.merge(L1, L2))

    def addB(st, c0, n, ps, pb):
        keys = [('B', st, c) for c in range(c0 // 256, (c0 + n) // 256)]
        tt(Bx[:, st, c0:c0 + n], Bx[:, st, c0:c0 + n], ps, ALU.add, keys + [pb], keys)

    def load_x(ti):
        for st in range(4):
            dma('sp', Bx[:, st, :], xs[ti * T + st * 128:ti * T + (st + 1) * 128, :], [], bkeys(st))

    def tile_full(ti, oi):
        fenceC()
        load_x(ti)
        rmsnorm_T(4, lambda st: Bx[:, st, :], lambda st: bkeys(st), C_GMIX)
        fenceB()
        fenceC()
        gates_tile(True)
        for h in range(4):
            gla_head(h, ti, True)
        for h in range(4):
            mlstm_head(h, ti, True, False, None)
        fenceB()
        load_x(ti)
        for cb in range(16):
            slot, skey = load_w(w_out[:, cb * 256:(cb + 1) * 256].rearrange("(k p) n -> p k n", p=128), 32, 256)
            for st in range(4):
                pb = P.rot('pj', ['pj0', 'pj1'])
                for kc in range(32):
                    mm(PS[pb][:, 0:256], mixedT[:, kc, st * 128:(st + 1) * 128], slot[:, kc, :], kc == 0, kc == 31,
                       [skey, ('mixT', kc // 4)], [pb])
                addB(st, cb * 256, 256, PS[pb][:, 0:256], pb)
        if stop != 'mix':
          tile_cross()
        if stop not in ('mix', 'cross'):
          tile_ffn()
        tile_final(oi)

    def tile_cross():
        fenceC()
        rmsnorm_T(4, lambda st: Bx[:, st, :], lambda st: bkeys(st), C_GCR)
        fenceC()
        qc = Cc[:, 0:4096].rearrange("p (a b) -> p a b", a=8)
        ocT = Cc[:, 4096:8192].rearrange("p (a b) -> p a b", a=8)
        KTh = Cc[:, 8192:10240].rearrange("p (a b) -> p a b", a=8)
        Vh = Cc[:, 10240:12288].rearrange("p (a b) -> p a b", a=2)
        eT = Cc[:, 12288:13312].rearrange("p (a b) -> p a b", a=2)
        rden = Cc[:, 13312:14336].bitcast(F32)
        for h in range(4):
            def ev_q(m, ps, pb):
                cp(qc[:, m, :], ps, [pb], ['qc'], eng='act' if m % 2 else 'dve')
            proj_fm(wq, h * 1024, 8, ev_q)
            dma('sp', KTh, KT_d[h * 8:(h + 1) * 8].rearrange("k p m -> p k m"), ['KT_d'], ['KTh'])
            for mc in range(2):
                dma('sp', Vh[:, mc, :], V_d[mc][:, h * 1024:(h + 1) * 1024], ['V_d'], ['Vh'])
            for mc in range(2):
                pb = ['mS', 'mO'][mc]
                for dc in range(8):
                    mm(PS[pb][:, :], KTh[:, dc, mc * 128:(mc + 1) * 128], qc[:, dc, :], dc == 0, dc == 7,
                       ['KTh', 'qc'], [pb])
                act(eT[:, mc, :], PS[pb][:, :], AF.Exp, [pb], ['eT'], scale=1.0 / 32.0)
            for mc in range(2):
                mm(PS['mA'][:, :], ONESB[:, :], eT[:, mc, :], mc == 0, mc == 1, ['cst', 'eT'], ['mA'])
            P.op('dve', lambda e, a=rden: e.reciprocal(out=a, in_=PS['mA'][:, :]), ['mA'], ['rden'])
            for dc in range(8):
                pb = P.rot('pj', ['pj0', 'pj1'])
                for mc in range(2):
                    mm(PS[pb][:, :], Vh[:, mc, dc * 128:(dc + 1) * 128], eT[:, mc, :], mc == 0, mc == 1,
                       ['Vh', 'eT'], [pb])
                tt(ocT[:, dc, :], PS[pb][:, :], rden, ALU.mult, [pb, 'rden'], ['ocT'])
            for cb in range(4):
                slot, skey = load_w(wo[h * 1024:(h + 1) * 1024, cb * 1024:(cb + 1) * 1024].rearrange("(k p) n -> p k n", p=128), 8, 1024)
                for st in range(4):
                    for hf in range(2):
                        pb = P.rot('pj', ['pj0', 'pj1'])
                        for kc in range(8):
                            mm(PS[pb][:, :], ocT[:, kc, st * 128:(st + 1) * 128], slot[:, kc, hf * 512:(hf + 1) * 512],
                               kc == 0, kc == 7, [skey, 'ocT'], [pb])
                        addB(st, cb * 1024 + hf * 512, 512, PS[pb][:, :], pb)
    def tile_ffn():
        fenceC()
        rmsnorm_T(4, lambda st: Bx[:, st, :], lambda st: bkeys(st), C_GFFN)
        fenceC()
        hidT = Cc[:, 0:11264].rearrange("p (a b) -> p a b", a=22)
        sg = [Cc[:, 11264 + i * 1024:11264 + (i + 1) * 1024].bitcast(F32) for i in range(2)]
        for (hc0, nch) in HID_BLOCKS:
            ci = 0
            while ci < nch:
                nm = min(2, nch - ci)
                col = (hc0 + ci) * 128
                sg_, kg = load_w(w_gate[:, col:col + nm * 128].rearrange("(k p) n -> p k n", p=128), 32, nm * 128)
                su_, ku = load_w(w_up[:, col:col + nm * 128].rearrange("(k p) n -> p k n", p=128), 32, nm * 128)
                for mi in range(nm):
                    i = (ci + mi) % 2
                    bg, bu = [('mS', 'mO'), ('mA', 'mU')][i]
                    for kc in range(32):
                        mm(PS[bg][:, :], sg_[:, kc, mi * 128:(mi + 1) * 128], nT[:, kc, :], kc == 0, kc == 31,
                           [kg] + NTK, [bg])
                    for kc in range(32):
                        mm(PS[bu][:, :], su_[:, kc, mi * 128:(mi + 1) * 128], nT[:, kc, :], kc == 0, kc == 31,
                           [ku] + NTK, [bu])
                    act(sg[i], PS[bg][:, :], AF.Silu, [bg], [('sg', i)])
                    tt(hidT[:, ci + mi, :], sg[i], PS[bu][:, :], ALU.mult, [('sg', i), bu], ['hidT'])
                ci += nm
            for cb in range(16):
                slot, skey = load_w(w_down[hc0 * 128:(hc0 + nch) * 128, cb * 256:(cb + 1) * 256].rearrange("(k p) n -> p k n", p=128), nch, 256)
                for st in range(4):
                    pb = P.rot('pj', ['pj0', 'pj1'])
                    for kc in range(nch):
                        mm(PS[pb][:, 0:256], hidT[:, kc, st * 128:(st + 1) * 128], slot[:, kc, :], kc == 0, kc == nch - 1,
                           [skey, 'hidT'], [pb])
                    addB(st, cb * 256, 256, PS[pb][:, 0:256], pb)
    def tile_final(oi):
        fenceC()
        dma('sp', gfin_sb, gfin_d[:, :], [], ['gfin'])
        for st in range(4):
            ssv = SM[:, 2 * st:2 * st + 1]
            rsv = SM[:, 2 * st + 1:2 * st + 2]
            ms(ssv, 0.0, [('ss', st)])
            act(junk, Bx[:, st, :], AF.Square, bkeys(st), [('ss', st)], accum=ssv)
            ts(rsv, ssv, 1.0 / D, EPS, ALU.mult, ALU.add, [('ss', st)], [('rs', st)])
            act(rsv, rsv, AF.Ln, [('rs', st)], [('rs', st)])
            act(rsv, rsv, AF.Exp, [('rs', st)], [('rs', st)], scale=-0.5)
            stt(Bx[:, st, :], Bx[:, st, :], rsv, gfin_sb, ALU.mult, ALU.mult, bkeys(st) + [('rs', st), 'gfin'], bkeys(st))
            dma('sp', out[oi * T + st * 128:oi * T + (st + 1) * 128, :], Bx[:, st, :], bkeys(st), [('out', oi, st)])

    def tile_state(ti, want_q_halo):
        fenceC()
        load_x(ti)
        rmsnorm_T(4, lambda st: Bx[:, st, :], lambda st: bkeys(st), C_GMIX)
        fenceB()
        gates_tile(False)
        state_mixers(ti, want_q_halo)
        fenceB()

    TEMP_KEYS = (['kT_f', 'qT_f', 'xk', 'xq', 'S_f', 'S_bf', 'n_f', 'n_bf', 'ef', 'nflag', 'k_bf', 'q_bf']
                 + [(n, s_) for n in ('v_bf', 'gs', 'sp', 'E1', 'E2', 'ktl', 'qtl', 'kh', 'AT', 'ssq', 'rsq', 'dn', 'gi',
                                     'spf', 'colb', 'colbD', 'wexp', 'EB', 'DT') for s_ in range(4)]
                 + [(n, s_) for n in ('khT', 'og', 'elog', 'Lf') for s_ in range(2)]
                 + [('skT', 0), ('skT', 1), ('sxk', 0), ('sxk', 1), 'xq_st']
                 + [(n, a_, b_) for n in ('sv', 'ssp') for a_ in range(2) for b_ in range(4)])
    ALLB = [k for st in range(4) for k in bkeys(st)]

    CKEYS = [('mixT', i) for i in range(8)] + ['qc', 'ocT', 'KTh', 'Vh', 'eT', 'rden', 'hidT', ('sg', 0), ('sg', 1),
             'gfin', ('xn', 0), ('xn', 1), ('ktb', 0), ('ktb', 1), ('vtb', 0), ('vtb', 1)]

    def fenceB():
        P.op('dve', lambda e: e.memset(SM[:, 100:101], 0.0), [], ALLB + TEMP_KEYS)

    def fenceC():
        P.op('dve', lambda e: e.memset(SM[:, 101:102], 0.0), [], CKEYS)

    phase_mem()
    for ti in range(NT):
        if ti < npre:
            tile_state(ti, last_pre_q and ti == npre - 1)
        else:
            tile_full(ti, ti - npre)
    P.emit()
    nc._prog_stats = P.stats
    return nc


def _pack_consts(inp, flags):
    c = np.zeros((128, NCST), np.float32)
    c[:, C_ID:C_ID + 128] = np.eye(128, dtype=np.float32)
    tri = np.triu(np.ones((128, 128), np.float32))
    c[:, C_UG:C_UG + 128] = tri * (-1.0 / 16.0)
    c[:, C_UN:C_UN + 128] = -tri
    c[:, C_TRI:C_TRI + 128] = tri
    c[:, C_ONE:C_ONE + 128] = 1.0

    def fm(v):
        return np.ascontiguousarray(np.asarray(v, np.float32).reshape(32, 128).T)
    c[:, C_GMIX:C_GMIX + 32] = fm(inp["norm_mix_g"][0])
    c[:, C_GCR:C_GCR + 32] = fm(inp["norm_cross_g"][0])
    c[:, C_GFFN:C_GFFN + 32] = fm(inp["norm_ffn_g"][0])
    c[:, C_GMEM:C_GMEM + 32] = fm(inp["norm_mem_g"][0])
    gn = np.concatenate([np.asarray(inp["gla_norm_g"][0]).reshape(-1), np.asarray(inp["mlstm_norm_g"][0]).reshape(-1)])
    c[:, C_GNORM:C_GNORM + 32] = fm(gn)
    cw = np.asarray(inp["mlstm_conv_w"][0], np.float32)
    c[:, C_CW:C_CW + 64] = cw.T.reshape(16, 128, 4).transpose(1, 0, 2).reshape(128, 64)
    c[:, C_CB:C_CB + 16] = np.asarray(inp["mlstm_conv_b"][0], np.float32).reshape(16, 128).T
    c[:, C_BIF:C_BIF + 4] = np.asarray(inp["mlstm_igate_b"][0], np.float32)[None, :]
    c[:, C_BIF + 4:C_BIF + 8] = np.asarray(inp["mlstm_fgate_b"][0], np.float32)[None, :]
    c[:, C_FLAG:C_FLAG + len(flags)] = np.asarray(flags, np.float32)[None, :]
    return c


def _shared_maps(inp):
    w2 = np.zeros((32, 1024), np.float32)
    w2[0:16] = np.asarray(inp["gla_gate_w2"][0], np.float32)
    w2[16] = np.asarray(inp["gla_gate_b"][0], np.float32)
    m = {
        "w_in": np.ascontiguousarray(inp["w_in"][0]), "w_out": np.ascontiguousarray(inp["w_out"][0]),
        "wq": np.ascontiguousarray(inp["wq_c"][0]), "wk": np.ascontiguousarray(inp["wk_c"][0]),
        "wv": np.ascontiguousarray(inp["wv_c"][0]), "wo": np.ascontiguousarray(inp["wo_c"][0]),
        "w_gate": np.ascontiguousarray(inp["w_gate"][0]), "w_up": np.ascontiguousarray(inp["w_up"][0]),
        "w_down": np.ascontiguousarray(inp["w_down"][0]),
        "w2aug": w2,
        "gfin": np.ascontiguousarray(np.broadcast_to(np.asarray(inp["norm_final_g"], np.float32)[None, :], (128, D))),
    }
    return m


NPRE_FULL = 12
NOWN_FULL = 4


def kernel(**inputs):
    inp = {k: np.asarray(v) for k, v in inputs.items()}
    x = inp["x"]
    mem = inp["mem"]
    nc = build(NPRE_FULL, NOWN_FULL)
    shared = _shared_maps(inp)
    in_maps = []
    for c in range(8):
        b, j = c // 4, c % 4
        nz = (NPRE_FULL - 4 * j) * T
        xs = np.concatenate([np.zeros((nz, D), np.float32), x[b, 0:(4 * j + 4) * T]], axis=0)
        flags = [1.0 if i >= NPRE_FULL - 4 * j else 0.0 for i in range(NPRE_FULL)]
        m = dict(shared)
        m["xs"] = np.ascontiguousarray(xs)
        m["mem"] = np.ascontiguousarray(mem[b])
        m["cst"] = _pack_consts(inp, flags)
        in_maps.append(m)
    res = run_bass_kernel_spmd(nc, in_maps, core_ids=list(range(8)))
    outp = np.zeros((2, 8192, D), np.float32)
    for c in range(8):
        b, j = c // 4, c % 4
        outp[b, j * 2048:(j + 1) * 2048] = res.results[c]["out"]
    return outp
```

```python
import numpy as np
import concourse.bass as bass
import concourse.mybir as mybir
from concourse.bass_utils import run_bass_kernel_spmd

F32 = mybir.dt.float32
BF16 = mybir.dt.bfloat16
AF = mybir.ActivationFunctionType
ALU = mybir.AluOpType

D = 4096
T = 512
EPS = 1e-6
NSLOT = 4
SLOT = 8192
FH = 11008
HID_BLOCKS = [(0, 22), (22, 22), (44, 21), (65, 21)]

C_ID, C_UG, C_UN, C_TRI, C_ONE = 0, 128, 256, 384, 512
C_GMIX, C_GCR, C_GFFN, C_GMEM, C_GNORM = 640, 672, 704, 736, 768
C_CW, C_CB, C_BIF, C_FLAG = 800, 864, 880, 888
NCST = 904
LN16 = float(np.log(1.0 / 16.0))


class Prog:
    EPOCH = 30000
    KDMA = 8

    def __init__(self, nc):
        self.nc = nc
        self.eng = {'pe': nc.tensor, 'act': nc.scalar, 'dve': nc.vector, 'pool': nc.gpsimd, 'sp': nc.sync}
        self.ops = []
        self.last_w = {}
        self.readers = {}
        self.rotc = {}
        self.cap = None

    def begin(self):
        self.cap = []

    def mark(self):
        if self.cap is not None:
            self.cap.append(None)

    def end(self):
        c, self.cap = self.cap, None
        return c

    def commit(self, lst):
        for t in lst:
            if t is not None:
                self.op(*t)

    @staticmethod
    def merge(a, b):
        def segs(l):
            out, cur = [], []
            for t in l:
                if t is None:
                    if cur:
                        out.append(cur)
                    cur = []
                else:
                    cur.append(t)
            if cur:
                out.append(cur)
            return out
        sa, sb = segs(a), segs(b)
        res = []
        i = j = 0
        na, nb = len(sa), len(sb)
        while i < na or j < nb:
            if j < nb and (i >= na or j * na <= i * nb):
                res.extend(sb[j])
                j += 1
            else:
                res.extend(sa[i])
                i += 1
        return res

    def op(self, eng, fn, r=(), w=(), dma=False):
        if self.cap is not None:
            self.cap.append((eng, fn, list(r), list(w), dma))
            return None
        i = len(self.ops)
        deps = set()
        for k in r:
            lw = self.last_w.get(k)
            if lw is not None:
                deps.add(lw)
        for k in w:
            lw = self.last_w.get(k)
            if lw is not None:
                deps.add(lw)
            rd = self.readers.get(k)
            if rd:
                deps.update(rd[0].values())
                deps.update(rd[1])
        for k in r:
            rd = self.readers.get(k)
            if rd is None:
                rd = self.readers[k] = ({}, [])
            if dma:
                rd[1].append(i)
            else:
                rd[0][eng] = i
        for k in w:
            self.last_w[k] = i
            self.readers[k] = ({}, [])
        deps.discard(i)
        latest = {}
        fd = []
        for d in deps:
            o = self.ops[d]
            if o['dma']:
                fd.append(d)
                continue
            if o['eng'] == 'pe' and eng == 'pe':
                continue
            if o['eng'] not in latest or latest[o['eng']] < d:
                latest[o['eng']] = d
        fd.extend(latest.values())
        self.ops.append({'eng': eng, 'fn': fn, 'deps': fd, 'dma': dma, 'inc': False})
        return i

    def rot(self, grp, names):
        c = self.rotc.get(grp, 0)
        self.rotc[grp] = c + 1
        return names[c % len(names)]

    def emit(self):
        nc = self.nc
        ops = self.ops
        for o in ops:
            for d in o['deps']:
                ops[d]['inc'] = True
        cnt = {e: 0 for e in self.eng}
        dcnt = {e: 0 for e in self.eng}
        sems = {}

        def getsem(name):
            if name not in sems:
                sems[name] = nc.alloc_semaphore(name)
            return sems[name]
        comp = [None] * len(ops)
        pre = [None] * len(ops)
        for i, o in enumerate(ops):
            e = o['eng']
            if o['dma']:
                j = dcnt[e]
                dcnt[e] += 1
                nm = "d_%s_%d" % (e, j % self.KDMA)
                comp[i] = (nm, 16 * (j // self.KDMA + 1))
                if j >= self.KDMA:
                    pre[i] = (nm, 16 * (j // self.KDMA))
            elif o['inc']:
                c = cnt[e]
                cnt[e] += 1
                comp[i] = ("c_%s_%d" % (e, c // self.EPOCH), c % self.EPOCH + 1)
        known = {e: {} for e in self.eng}
        final = {}
        nwait = 0
        for i, o in enumerate(ops):
            e = o['eng']
            h = self.eng[e]
            waits = [comp[d] for d in o['deps']]
            if pre[i] is not None:
                waits.append(pre[i])
            kn = known[e]
            best = {}
            for (nm, v) in waits:
                if kn.get(nm, 0) >= v:
                    continue
                if nm not in best or best[nm] < v:
                    best[nm] = v
            for nm, v in best.items():
                h.wait_ge(getsem(nm), v)
                kn[nm] = v
                nwait += 1
            ins = o['fn'](h)
            if comp[i] is not None:
                nm, v = comp[i]
                ins.then_inc(getsem(nm), 16 if o['dma'] else 1)
                if o['dma']:
                    final[nm] = v
        for nm, v in final.items():
            nc.sync.wait_ge(getsem(nm), v)
        self.stats = {'ops': len(ops), 'waits': nwait, 'cnt': cnt, 'dcnt': dcnt, 'sems': len(sems)}


def build(npre, nown, last_pre_q=True, stop=None, dbg=False):
    nc = bass.Bass("TRN2", target_bir_lowering=False)
    P = Prog(nc)
    NT = npre + nown

    def din(name, shape, dt=F32):
        return nc.dram_tensor(name, shape, dt, kind="ExternalInput").ap()
    xs = din("xs", [NT * T, D])
    mem = din("mem", [256, D])
    w_in = din("w_in", [D, 12312])
    w_out = din("w_out", [D, D])
    wq = din("wq", [D, D])
    wk = din("wk", [D, D])
    wv = din("wv", [D, D])
    wo = din("wo", [D, D])
    w_gate = din("w_gate", [D, FH])
    w_up = din("w_up", [D, FH])
    w_down = din("w_down", [FH, D])
    cst_d = din("cst", [128, NCST])
    w2_d = din("w2aug", [32, 1024])
    gfin_d = din("gfin", [128, D])
    out = nc.dram_tensor("out", [nown * T, D], F32, kind="ExternalOutput").ap()

    def dscr(name, shape, dt):
        return nc.dram_tensor(name, shape, dt, kind="Internal").ap()
    st_gla = dscr("st_gla", [4, 128, 1024], F32)
    st_mC = dscr("st_mC", [4, 128, 1024], F32)
    st_mn = dscr("st_mn", [4, 128, 2], F32)
    KT_d = dscr("KT_d", [32, 128, 256], BF16)
    V_d = dscr("V_d", [2, 128, D], BF16)

    A = nc.alloc_sbuf_tensor("A", [128, 16384], BF16)
    B = nc.alloc_sbuf_tensor("B", [128, 16384], F32)
    Cc = nc.alloc_sbuf_tensor("C", [128, 16384], BF16)
    Wt = [nc.alloc_sbuf_tensor("W%d" % i, [128, SLOT], BF16) for i in range(NSLOT)]
    CST = nc.alloc_sbuf_tensor("CST", [128, NCST], F32)
    W2 = nc.alloc_sbuf_tensor("W2AUG", [32, 1024], F32)
    WAG = nc.alloc_sbuf_tensor("WAG", [128, 512], BF16)
    WIF = nc.alloc_sbuf_tensor("WIF", [128, 256], BF16)
    HAL = nc.alloc_sbuf_tensor("HAL", [128, 48], F32)
    SM = nc.alloc_sbuf_tensor("SM", [128, 128], F32)
    ONESB = nc.alloc_sbuf_tensor("ONESB", [128, 128], BF16)
    AGT = nc.alloc_sbuf_tensor("AGT", [32, 512], F32)
    PS = {n: nc.alloc_psum_tensor(n, [128, 512], F32) for n in
          ['pj0', 'pj1', 'pt0', 'pt1', 'mA', 'mS', 'mO', 'mU']}

    nT = A[:, :].rearrange("p (k t) -> p k t", k=32)
    Bx = B[:, :].rearrange("p (s c) -> p s c", s=4)
    ident = CST[:, C_ID:C_ID + 128]
    Ugla = CST[:, C_UG:C_UG + 128]
    Uneg = CST[:, C_UN:C_UN + 128]
    tri = CST[:, C_TRI:C_TRI + 128]
    onesf = CST[:, C_ONE:C_ONE + 128]

    def mm(out_, lhsT, rhs, start, stop, r, w):
        return P.op('pe', lambda e, a=out_, b=lhsT, c=rhs, s=start, t=stop: e.matmul(a, lhsT=b, rhs=c, start=s, stop=t), r, w)

    def tr(out_, in_, r, w):
        return P.op('pe', lambda e, a=out_, b=in_: e.transpose(a, b, ident), list(r) + ['cst'], w)

    def act(out_, in_, func, r, w, bias=None, scale=None, accum=None):
        kw = {}
        if bias is not None:
            kw['bias'] = bias
        if scale is not None:
            kw['scale'] = scale
        if accum is not None:
            kw['accum_out'] = accum
        return P.op('act', lambda e, a=out_, b=in_, f=func, k=kw: e.activation(out=a, in_=b, func=f, **k), r, w)

    def tt(out_, in0, in1, op_, r, w, eng='dve'):
        return P.op(eng, lambda e, a=out_, b=in0, c=in1, o=op_: e.tensor_tensor(out=a, in0=b, in1=c, op=o), r, w)

    def ts(out_, in0, s1, s2, op0, op1, r, w, eng='dve'):
        if s2 is None:
            return P.op(eng, lambda e, a=out_, b=in0, c=s1, o=op0: e.tensor_single_scalar(out=a, in_=b, scalar=c, op=o), r, w)
        return P.op(eng, lambda e, a=out_, b=in0, c=s1, d=s2, o=op0, q=op1: e.tensor_scalar(out=a, in0=b, scalar1=c, scalar2=d, op0=o, op1=q), r, w)

    def stt(out_, in0, sc, in1, op0, op1, r, w, eng='dve'):
        return P.op(eng, lambda e, a=out_, b=in0, c=sc, d=in1, o=op0, q=op1: e.scalar_tensor_tensor(out=a, in0=b, scalar=c, in1=d, op0=o, op1=q), r, w)

    def cp(out_, in_, r, w, eng='dve'):
        if eng == 'act':
            return P.op(eng, lambda e, a=out_, b=in_: e.activation(out=a, in_=b, func=AF.Copy), r, w)
        return P.op(eng, lambda e, a=out_, b=in_: e.tensor_copy(out=a, in_=b), r, w)

    def ms(ap, val, w, eng='dve'):
        return P.op(eng, lambda e, a=ap, v=val: e.memset(a, v), [], w)

    def dma(q, out_, in_, r, w):
        return P.op(q, lambda e, a=out_, b=in_: e.dma_start(out=a, in_=b), r, w, dma=True)

    def dump(name, ap, keys):
        shp = list(ap.shape)
        d = nc.dram_tensor("dbg_" + name, shp, F32, kind="ExternalOutput").ap()
        dma('pool', d, ap, keys, [('dbg', name)])

    def load_w(src, a, b):
        c = P.rotc.get('wslot', 0)
        P.rotc['wslot'] = c + 1
        si = c % NSLOT
        view = Wt[si][:, 0:a * b].rearrange("p (a b) -> p a b", a=a)
        key = ('W', si)
        dma('pool', view, src, [], [key])
        return view, key

    def bc3(ap2, n):
        return ap2.unsqueeze(2).broadcast_to([128, ap2.shape[1], n])

    dma('sp', CST[:, :], cst_d[:, :], [], ['cst'])
    dma('sp', W2[:, :], w2_d[:, :], [], ['w2'])
    dma('pool', WAG[:, :].rearrange("p (k n) -> p k n", k=32),
        w_in[:, 6144:6160].rearrange("(k p) n -> p k n", p=128), [], ['wag'])
    dma('pool', WIF[:, :].rearrange("p (k n) -> p k n", k=32),
        w_in[:, 12304:12312].rearrange("(k p) n -> p k n", p=128), [], ['wif'])
    ms(HAL[:, :], 0.0, ['hal'])
    ms(AGT[:, :], 1.0, ['agt'])
    ms(ONESB[:, :], 1.0, ['cst'])
    WAG3 = WAG[:, :].rearrange("p (k n) -> p k n", k=32)
    WIF3 = WIF[:, :].rearrange("p (k n) -> p k n", k=32)

    junk = Cc[:, 0:4096]
    gfin_sb = Cc[:, 4096:12288].bitcast(F32)
    XN = [Cc[:, 12288 + i * 1024:12288 + (i + 1) * 1024].bitcast(F32) for i in range(2)]
    mixedT = Cc[:, :].rearrange("p (k t) -> p k t", k=32)

    def rmsnorm_T(nst, src, skeys, gcol, ncols_tok=512):
        for st in range(nst):
            ssv = SM[:, 2 * st:2 * st + 1]
            rsv = SM[:, 2 * st + 1:2 * st + 2]
            ms(ssv, 0.0, [('ss', st)])
            act(junk, src(st), AF.Square, skeys(st), [('ss', st)], accum=ssv)
            ts(rsv, ssv, 1.0 / D, EPS, ALU.mult, ALU.add, [('ss', st)], [('rs', st)])
            act(rsv, rsv, AF.Ln, [('rs', st)], [('rs', st)])
            act(rsv, rsv, AF.Exp, [('rs', st)], [('rs', st)], scale=-0.5)
            for cb in range(8):
                xn = XN[cb % 2]
                act(xn, src(st)[:, cb * 512:(cb + 1) * 512], AF.Copy, list(skeys(st)) + [('rs', st)], [('xn', cb % 2)], scale=rsv)
                pb = P.rot('pt', ['pt0', 'pt1'])
                for j in range(4):
                    tr(PS[pb][:, j * 128:(j + 1) * 128], xn[:, j * 128:(j + 1) * 128], [('xn', cb % 2)], [pb])
                tt(nT[:, cb * 4:(cb + 1) * 4, st * 128:(st + 1) * 128],
                   PS[pb][:, :].rearrange("p (a b) -> p a b", a=4),
                   bc3(CST[:, gcol + cb * 4:gcol + cb * 4 + 4], 128), ALU.mult,
                   [pb, 'cst'], [('nT', st)])

    def bkeys(st, c0=0, c1=16):
        return [('B', st, c) for c in range(c0, c1)]

    NTK = [('nT', s) for s in range(4)]

    def proj_fm(wd, c0, nch, evac, ntok=512, nk=None):
        nk = nk or NTK
        m = 0
        while m < nch:
            nm = min(2, nch - m)
            slot, skey = load_w(wd[:, c0 + m * 128:c0 + (m + nm) * 128].rearrange("(k p) n -> p k n", p=128), 32, nm * 128)
            for mi in range(nm):
                pb = P.rot('pj', ['pj0', 'pj1'])
                for kc in range(32):
                    mm(PS[pb][:, 0:ntok], slot[:, kc, mi * 128:(mi + 1) * 128], nT[:, kc, 0:ntok], kc == 0, kc == 31,
                       [skey] + nk, [pb])
                evac(m + mi, PS[pb][:, 0:ntok], pb)
            m += nm

    def proj_tm(wd, c0, npiece, evac, nst=4):
        for pc in range(npiece):
            slot, skey = load_w(wd[:, c0 + pc * 256:c0 + (pc + 1) * 256].rearrange("(k p) n -> p k n", p=128), 32, 256)
            for st in range(nst):
                pb = P.rot('pj', ['pj0', 'pj1'])
                for kc in range(32):
                    mm(PS[pb][:, 0:256], nT[:, kc, st * 128:(st + 1) * 128], slot[:, kc, :], kc == 0, kc == 31,
                       [skey, ('nT', st)], [pb])
                evac(pc, st, PS[pb][:, 0:256], pb)

    def phase_mem():
        for st in range(2):
            dma('sp', Bx[:, st, :], mem[st * 128:(st + 1) * 128, :], [], bkeys(st))
        rmsnorm_T(2, lambda st: Bx[:, st, :], lambda st: bkeys(st), C_GMEM)
        ktb = [Cc[:, 14336 + i * 256:14336 + (i + 1) * 256] for i in range(2)]

        def ev_k(m, ps, pb):
            i = m % 2
            cp(ktb[i], ps, [pb], [('ktb', i)], eng='act')
            dma('sp', KT_d[m], ktb[i], [('ktb', i)], ['KT_d'])
        proj_fm(wk, 0, 32, ev_k, ntok=256, nk=[('nT', 0), ('nT', 1)])
        vtb = [Cc[:, 14848 + i * 256:14848 + (i + 1) * 256] for i in range(2)]

        def ev_v(pc, st, ps, pb):
            i = (pc * 2 + st) % 2
            cp(vtb[i], ps, [pb], [('vtb', i)], eng='dve')
            dma('sp', V_d[st][:, pc * 256:(pc + 1) * 256], vtb[i], [('vtb', i)], ['V_d'])
        proj_tm(wv, 0, 16, ev_v, nst=2)

    def bsub(off, n):
        return B[:, off:off + n]
    qT_f = bsub(0, 1024).rearrange("p (a b) -> p a b", a=2)
    kT_f = bsub(1024, 1024).rearrange("p (a b) -> p a b", a=2)
    v_bf2 = [bsub(2048 + i * 1024, 1024).bitcast(BF16).rearrange("p (a b) -> p a b", a=4) for i in range(2)]
    gs_b2 = [bsub(4096 + i * 1024, 1024).bitcast(BF16).rearrange("p (a b) -> p a b", a=4) for i in range(2)]
    sp_f = bsub(6144, 1024).rearrange("p (a b) -> p a b", a=4)
    qk_bf = bsub(6144, 1024).bitcast(BF16)
    q_bf = qk_bf[:, 0:1024].rearrange("p (a b) -> p a b", a=2)
    k_bf = qk_bf[:, 1024:2048].rearrange("p (a b) -> p a b", a=2)
    xq = bsub(7168, 1030).rearrange("p (a b) -> p a b", a=2)
    xk = bsub(8198, 1030).rearrange("p (a b) -> p a b", a=2)
    R0 = 9240
    E1a = bsub(R0, 1024).rearrange("p (a b) -> p a b", a=2)
    E2a = bsub(R0 + 1024, 1024).rearrange("p (a b) -> p a b", a=2)
    DTa = bsub(R0, 512).rearrange("p (a b) -> p a b", a=4)
    EBa = bsub(R0 + 512, 512).rearrange("p (a b) -> p a b", a=4)
    Lf2 = [bsub(R0 + 1024 + i * 128, 128) for i in range(2)]
    qtla = bsub(R0 + 2048, 512).bitcast(BF16).rearrange("p (a b) -> p a b", a=2)
    ktla = bsub(R0 + 2560, 512).bitcast(BF16).rearrange("p (a b) -> p a b", a=2)
    khT2 = [bsub(R0 + 3072 + i * 256, 256).rearrange("p (a b) -> p a b", a=2) for i in range(2)]
    kha = bsub(R0 + 3584, 512).bitcast(BF16).rearrange("p (a b) -> p a b", a=4)
    ATa = bsub(R0 + 4096, 256).bitcast(BF16).rearrange("p (a b) -> p a b", a=4)
    og1 = bsub(R0 + 4352, 512)
    og2 = [bsub(R0 + 1024, 512), bsub(R0 + 1536, 512)]
    S_f = bsub(R0 + 4864, 1024).rearrange("p (a b) -> p a b", a=2)
    S_bf = bsub(R0 + 5888, 512).bitcast(BF16).rearrange("p (a b) -> p a b", a=2)
    elog2 = [bsub(R0 + 6400 + i * 256, 256) for i in range(2)]
    junk2 = bsub(R0 + 6400, 256).bitcast(BF16)
    SB0 = R0 + 6912
    n_f = bsub(SB0, 2)
    n_bf = bsub(SB0 + 2, 1).bitcast(BF16)
    sm2 = bsub(SB0 + 4, 32)
    gi_sb = bsub(SB0 + 36, 32).rearrange("p (a b) -> p a b", a=4)
    spf = bsub(SB0 + 68, 16).rearrange("p (a b) -> p a b", a=4)
    colb = bsub(SB0 + 84, 16).rearrange("p (a b) -> p a b", a=4)
    colbD = bsub(SB0 + 100, 16).rearrange("p (a b) -> p a b", a=4)
    ef_t = bsub(SB0 + 116, 4)
    wexpa = bsub(SB0 + 120, 4)
    nflag = bsub(SB0 + 124, 2)
    assert SB0 + 126 <= 16384

    HBL = [(PS['mA'][:, 0:256], 'mA'), (PS['mS'][:, 0:256], 'mS'), (PS['mU'][:, 0:256], 'mU')]

    def hb():
        return P.rot('hb', HBL)

    HB5 = [(PS['mA'][:, :], 'mA'), (PS['mS'][:, :], 'mS'), (PS['mU'][:, :], 'mU')]

    def hb512():
        return P.rot('hb', HB5)

    def gnorm_bc(c4):
        return bc3(CST[:, C_GNORM + c4:C_GNORM + c4 + 4], 128)

    def gates_tile(full):
        pb = P.rot('pj', ['pj0', 'pj1'])
        for kc in range(32):
            mm(PS[pb][0:16, :], WAG3[:, kc, :], nT[:, kc, :], kc == 0, kc == 31, ['wag'] + NTK, [pb])
        cp(AGT[0:16, :], PS[pb][0:16, :], [pb], ['agt'], eng='act')
        for st in range(4):
            pb = P.rot('pj', ['pj0', 'pj1'])
            for kc in range(32):
                mm(PS[pb][:, 0:8], nT[:, kc, st * 128:(st + 1) * 128], WIF3[:, kc, :], kc == 0, kc == 31,
                   ['wif', ('nT', st)], [pb])
            tt(gi_sb[:, st, :], PS[pb][:, 0:8], CST[:, C_BIF:C_BIF + 8], ALU.add, [pb, 'cst'], [('gi', st)])
            act(ef_t, gi_sb[:, st, 4:8], AF.Exp, [('gi', st)], ['ef'], scale=-1.0)
            act(spf[:, st, :], ef_t, AF.Ln, ['ef'], [('spf', st)], bias=1.0)
            hp, hk = hb()
            mm(hp[:, 0:4], Uneg, spf[:, st, :], True, True, ['cst', ('spf', st)], [hk])
            tt(colb[:, st, :], gi_sb[:, st, 0:4], hp[:, 0:4], ALU.subtract, [('gi', st), hk], [('colb', st)])
            ts(colbD[:, st, :], colb[:, st, :], LN16, None, ALU.add, None, [('colb', st)], [('colbD', st)])

    def conv_silu(xin, ch0, outf, key_in, key_out):
        for dc in range(2):
            ch = ch0 + dc
            wcol = C_CW + ch * 4
            ts(outf[:, dc, :], xin[:, dc, 0:512], CST[:, wcol:wcol + 1], CST[:, C_CB + ch:C_CB + ch + 1],
               ALU.mult, ALU.add, [key_in, 'cst'], [key_out])
            for j in range(1, 4):
                stt(outf[:, dc, :], xin[:, dc, j:j + 512], CST[:, wcol + j:wcol + j + 1], outf[:, dc, :],
                    ALU.mult, ALU.add, [key_in, 'cst', key_out], [key_out])
        act(outf[:, :, :], outf[:, :, :], AF.Silu, [key_out], [key_out])

    def halo_in(xin, ch0, key):
        cp(xin[:, :, 0:3], HAL[:, ch0 * 3:(ch0 + 2) * 3].rearrange("p (a b) -> p a b", a=2), ['hal'], [key], eng='dve')

    def halo_out(xin, ch0, key):
        cp(HAL[:, ch0 * 3:(ch0 + 2) * 3].rearrange("p (a b) -> p a b", a=2), xin[:, :, 512:515], [key], ['hal'], eng='dve')

    def sset(s_):
        base = s_ * 4104
        return {
            'kT': bsub(base, 1024).rearrange("p (a b) -> p a b", a=2),
            'v': bsub(base + 1024, 1024).bitcast(BF16).rearrange("p (a b) -> p a b", a=4),
            'sp': bsub(base + 2048, 1024).rearrange("p (a b) -> p a b", a=4),
            'xk': bsub(base + 3072, 1030).rearrange("p (a b) -> p a b", a=2),
        }
    SSET = [sset(0), sset(1)]
    xq_st = bsub(8208, 1030).rearrange("p (a b) -> p a b", a=2)
    HBS = [(PS['mA'][:, 0:256], 'mA'), (PS['mS'][:, 0:256], 'mS')]

    def hbs():
        return P.rot('hbs', HBS)

    def pfm(wd, c0, nch, evac):
        def ev(m, ps, pb):
            evac(m, ps, pb)
            P.mark()
        proj_fm(wd, c0, nch, ev)

    def ptm(wd, c0, npiece, evac):
        def ev(pc, st, ps, pb):
            evac(pc, st, ps, pb)
            P.mark()
        proj_tm(wd, c0, npiece, ev)

    def gla_sproj(h, s_):
        V = SSET[s_]

        def ev_k(m, ps, pb):
            cp(V['kT'][:, m, :], ps, [pb], [('skT', s_)], eng='act')
        pfm(w_in, 1024 + h * 256, 2, ev_k)

        def ev_v(pc, st, ps, pb):
            cp(V['v'][:, st, pc * 256:(pc + 1) * 256], ps, [pb], [('sv', s_, st)], eng='dve')
        ptm(w_in, 2048 + h * 512, 2, ev_v)

    def gla_srec(h, s_, ti):
        V = SSET[s_]
        kT, vb, sp = V['kT'], V['v'], V['sp']
        if ti == 0:
            ms(S_f[:, :, :], 0.0, ['S_f'])
        else:
            dma('sp', S_f[:, :, :], st_gla[h].rearrange("p (a b) -> p a b", a=2), [('st_gla', h)], ['S_f'])
        for st in range(4):
            hp, hk = hbs()
            mm(hp, AGT[0:32, st * 128:(st + 1) * 128], W2[0:32, h * 256:(h + 1) * 256], True, True, ['agt', 'w2'], [hk])
            el = elog2[st % 2]
            act(el, hp, AF.Exp, [hk], [('elog', st % 2)], scale=-1.0)
            act(sp[:, st, :], el, AF.Ln, [('elog', st % 2)], [('ssp', s_, st)], bias=1.0)
            P.mark()
        for st in range(4):
            sl = slice(st * 128, (st + 1) * 128)
            hp, hk = hbs()
            for dc in range(2):
                mm(hp[:, dc * 128:(dc + 1) * 128], sp[:, st, dc * 128:(dc + 1) * 128], Ugla, True, True,
                   [('ssp', s_, st), 'cst'], [hk])
            pbT = hp.rearrange("p (a b) -> p a b", a=2)
            act(E1a[:, :, sl], pbT, AF.Exp, [hk], [('E1', st)])
            act(E2a[:, :, sl], pbT, AF.Exp, [hk], [('E2', st)], scale=-1.0)
            P.mark()
        for st in range(4):
            sl = slice(st * 128, (st + 1) * 128)
            kt = khT2[st % 2]
            for dc in range(2):
                stt(kt[:, dc, :], kT[:, dc, sl], E1a[:, dc, st * 128 + 127:st * 128 + 128], E2a[:, dc, sl],
                    ALU.mult, ALU.mult, [('skT', s_), ('E1', st), ('E2', st)], [('khT', st % 2)])
            pb = P.rot('pt', ['pt0', 'pt1'])
            for dc in range(2):
                tr(PS[pb][:, dc * 128:(dc + 1) * 128], kt[:, dc, :], [('khT', st % 2)], [pb])
            cp(kha[:, st, :], PS[pb][:, 0:256], [pb], [('kh', st)], eng='act')
            P.mark()
        for st in range(4):
            for dc in range(2):
                pb2 = ['mO', 'mU'][dc]
                mm(PS[pb2][:, :], kha[:, st, dc * 128:(dc + 1) * 128], vb[:, st, :], True, True,
                   [('kh', st), ('sv', s_, st)], [pb2])
                stt(S_f[:, dc, :], S_f[:, dc, :], E1a[:, dc, st * 128 + 127:st * 128 + 128], PS[pb2][:, :],
                    ALU.mult, ALU.add, ['S_f', ('E1', st), pb2], ['S_f'])
            P.mark()
        dma('sp', st_gla[h].rearrange("p (a b) -> p a b", a=2), S_f[:, :, :], ['S_f'], [('st_gla', h)])

    def conv_silu_s(xin, ch0, outf, key_in, key_out):
        for dc in range(2):
            ch = ch0 + dc
            wcol = C_CW + ch * 4
            ts(outf[:, dc, :], xin[:, dc, 0:512], CST[:, wcol:wcol + 1], CST[:, C_CB + ch:C_CB + ch + 1],
               ALU.mult, ALU.add, [key_in, 'cst'], [key_out])
            for j in range(1, 4):
                stt(outf[:, dc, :], xin[:, dc, j:j + 512], CST[:, wcol + j:wcol + j + 1], outf[:, dc, :],
                    ALU.mult, ALU.add, [key_in, 'cst', key_out], [key_out])
        act(outf[:, :, :], outf[:, :, :], AF.Silu, [key_out], [key_out])

    def mlstm_sproj(h, s_, want_q_halo):
        V = SSET[s_]
        xk_ = V['xk']
        kx = ('sxk', s_)
        halo_in(xk_, 8 + h * 2, kx)

        def ev_k(m, ps, pb):
            cp(xk_[:, m, 3:515], ps, [pb], [kx], eng='act')
        pfm(w_in, 7184 + h * 256, 2, ev_k)
        conv_silu_s(xk_, 8 + h * 2, V['kT'], kx, ('skT', s_))
        halo_out(xk_, 8 + h * 2, kx)
        P.mark()

        def ev_v(pc, st, ps, pb):
            cp(V['v'][:, st, pc * 256:(pc + 1) * 256], ps, [pb], [('sv', s_, st)], eng='dve')
        ptm(w_in, 8208 + h * 512, 2, ev_v)
        if want_q_halo:
            halo_in(xq_st, h * 2, 'xq_st')

            def ev_q(m, ps, pb):
                cp(xq_st[:, m, 3:515], ps, [pb], ['xq_st'], eng='act')
            pfm(w_in, 6160 + h * 256, 2, ev_q)
            halo_out(xq_st, h * 2, 'xq_st')
            P.mark()

    def mlstm_srec(h, s_, ti, pslot):
        V = SSET[s_]
        km, vb = V['kT'], V['v']
        if ti == 0:
            ms(S_f[:, :, :], 0.0, ['S_f'])
            ms(n_f, 0.0, ['n_f'])
        else:
            dma('sp', S_f[:, :, :], st_mC[h].rearrange("p (a b) -> p a b", a=2), [('st_mC', h)], ['S_f'])
            dma('sp', n_f, st_mn[h], [('st_mn', h)], ['n_f'])
        for st in range(4):
            sl = slice(st * 128, (st + 1) * 128)
            Lf = Lf2[st % 2]
            ts(Lf, onesf, spf[:, st, h:h + 1], None, ALU.mult, None, ['cst', ('spf', st)], [('Lf', st % 2)])
            hp, hk = hbs()
            mm(hp[:, 0:128], Lf, Uneg, True, True, [('Lf', st % 2), 'cst'], [hk])
            act(EBa[:, st, :], hp[:, 0:128], AF.Exp, [hk], [('EB', st)])
            act(wexpa[:, st:st + 1], hp[:, 127:128], AF.Exp, [hk, ('colb', st)], [('wexp', st)], bias=colb[:, st, h:h + 1])
            P.mark()
            pb = P.rot('pt', ['pt0', 'pt1'])
            for dc in range(2):
                tr(PS[pb][:, dc * 128:(dc + 1) * 128], km[:, dc, sl], [('skT', s_)], [pb])
            ts(kha[:, st, :], PS[pb][:, 0:256], wexpa[:, st:st + 1], None, ALU.mult, None, [pb, ('wexp', st)], [('kh', st)])
            P.mark()
        for st in range(4):
            np_, nk = hbs()
            for dc in range(2):
                pb2 = ['mO', 'mU'][dc]
                mm(PS[pb2][:, :], kha[:, st, dc * 128:(dc + 1) * 128], vb[:, st, :], True, True,
                   [('kh', st), ('sv', s_, st)], [pb2])
                stt(S_f[:, dc, :], S_f[:, dc, :], EBa[:, st, 127:128], PS[pb2][:, :], ALU.mult, ALU.add,
                    ['S_f', ('EB', st), pb2], ['S_f'])
                mm(np_[:, dc:dc + 1], kha[:, st, dc * 128:(dc + 1) * 128], ONESB[:, 0:1], True, True,
                   [('kh', st), 'cst'], [nk])
            ts(nflag, np_[:, 0:2], CST[:, C_FLAG + pslot:C_FLAG + pslot + 1], None, ALU.mult, None,
               [nk, 'cst'], ['nflag'])
            stt(n_f, n_f, EBa[:, st, 127:128], nflag, ALU.mult, ALU.add, ['n_f', ('EB', st), 'nflag'], ['n_f'])
            P.mark()
        dma('sp', st_mC[h].rearrange("p (a b) -> p a b", a=2), S_f[:, :, :], ['S_f'], [('st_mC', h)])
        dma('sp', st_mn[h], n_f, ['n_f'], [('st_mn', h)])

    def state_mixers(ti, want_q_halo):
        def proj(i):
            P.begin()
            if i < 4:
                gla_sproj(i, i % 2)
            else:
                mlstm_sproj(i - 4, i % 2, want_q_halo)
            return P.end()

        def rec(i):
            P.begin()
            if i < 4:
                gla_srec(i, i % 2, ti)
            else:
                mlstm_srec(i - 4, i % 2, ti, ti)
            return P.end()
        P.commit(proj(0))
        for i in range(8):
            L1 = rec(i)
            L2 = proj(i + 1) if i < 7 else []
            P.commit(Prog.merge(L1, L2))

    def gla_fproj(h, s_, part):
        vb, gsb = v_bf2[s_], gs_b2[s_]
        if part == 'a1':
            def ev_v(pc, st, ps, pb):
                cp(vb[:, st, pc * 256:(pc + 1) * 256], ps, [pb], [('v_bf', s_, st)], eng='dve')
            ptm(w_in, 2048 + h * 512, 2, ev_v)
        elif part == 'a2':
            def ev_g(pc, st, ps, pb):
                act(gsb[:, st, pc * 256:(pc + 1) * 256], ps, AF.Silu, [pb], [('gs', s_, st)])
            ptm(w_in, 4096 + h * 512, 2, ev_g)
        else:
            def ev_k(m, ps, pb):
                cp(kT_f[:, m, :], ps, [pb], ['kT_f'], eng='act')
            pfm(w_in, 1024 + h * 256, 2, ev_k)

            def ev_q(m, ps, pb):
                act(qT_f[:, m, :], ps, AF.Copy, [pb], ['qT_f'], scale=1.0 / 16.0)
            pfm(w_in, h * 256, 2, ev_q)

    def gla_fpre(h, s_, ti):
        if ti == 0:
            ms(S_f[:, :, :], 0.0, ['S_f'])
        else:
            dma('sp', S_f[:, :, :], st_gla[h].rearrange("p (a b) -> p a b", a=2), [('st_gla', h)], ['S_f'])
        cp(S_bf[:, :, :], S_f[:, :, :], ['S_f'], ['S_bf'], eng='act')
        for st in range(4):
            hp, hk = hb()
            mm(hp, AGT[0:32, st * 128:(st + 1) * 128], W2[0:32, h * 256:(h + 1) * 256], True, True, ['agt', 'w2'], [hk])
            el = elog2[st % 2]
            act(el, hp, AF.Exp, [hk], [('elog', st % 2)], scale=-1.0)
            act(sp_f[:, st, :], el, AF.Ln, [('elog', st % 2)], [('sp', st)], bias=1.0)
            P.mark()
        for st in range(4):
            sl = slice(st * 128, (st + 1) * 128)
            hp, hk = hb()
            for dc in range(2):
                mm(hp[:, dc * 128:(dc + 1) * 128], sp_f[:, st, dc * 128:(dc + 1) * 128], Ugla, True, True,
                   [('sp', st), 'cst'], [hk])
            pbT = hp.rearrange("p (a b) -> p a b", a=2)
            act(E1a[:, :, sl], pbT, AF.Exp, [hk], [('E1', st)])
            act(E2a[:, :, sl], pbT, AF.Exp, [hk], [('E2', st)], scale=-1.0)
            P.mark()
        for st in range(4):
            sl = slice(st * 128, (st + 1) * 128)
            tt(ktla[:, :, sl], kT_f[:, :, sl], E2a[:, :, sl], ALU.mult, ['kT_f', ('E2', st)], [('ktl', st)])
            tt(qtla[:, :, sl], qT_f[:, :, sl], E1a[:, :, sl], ALU.mult, ['qT_f', ('E1', st)], [('qtl', st)])
            kt = khT2[st % 2]
            for dc in range(2):
                stt(kt[:, dc, :], kT_f[:, dc, sl], E1a[:, dc, st * 128 + 127:st * 128 + 128], E2a[:, dc, sl],
                    ALU.mult, ALU.mult, ['kT_f', ('E1', st), ('E2', st)], [('khT', st % 2)])
            pb = P.rot('pt', ['pt0', 'pt1'])
            for dc in range(2):
                tr(PS[pb][:, dc * 128:(dc + 1) * 128], kt[:, dc, :], [('khT', st % 2)], [pb])
            cp(kha[:, st, :], PS[pb][:, 0:256], [pb], [('kh', st)], eng='act')
            P.mark()
        for st in range(4):
            sl = slice(st * 128, (st + 1) * 128)
            hp, hk = hb()
            for dc in range(2):
                mm(hp[:, 0:128], ktla[:, dc, sl], qtla[:, dc, sl], dc == 0, dc == 1, [('ktl', st), ('qtl', st)], [hk])
            tt(ATa[:, st, :], hp[:, 0:128], tri, ALU.mult, [hk, 'cst'], [('AT', st)])
            P.mark()

    def head_norm_out(h8, s_, st, ob, dn):
        gsb = gs_b2[s_]
        ssq = sm2[:, st * 4:st * 4 + 1]
        rs = sm2[:, st * 4 + 1:st * 4 + 2]
        ms(ssq, 0.0, [('ssq', st)])
        if dn is None:
            act(junk2, PS[ob][:, :], AF.Square, [ob], [('ssq', st)], accum=ssq)
        else:
            act(junk2, PS[ob][:, :], AF.Square, [ob, ('dn', st)], [('ssq', st)], accum=ssq, scale=dn)
        ts(rs, ssq, 1.0 / 512, EPS, ALU.mult, ALU.add, [('ssq', st)], [('rsq', st)])
        act(rs, rs, AF.Ln, [('rsq', st)], [('rsq', st)])
        act(rs, rs, AF.Exp, [('rsq', st)], [('rsq', st)], scale=-0.5)
        if dn is not None:
            tt(rs, rs, dn, ALU.mult, [('rsq', st), ('dn', st)], [('rsq', st)])
        stt(og2[st % 2], PS[ob][:, :], rs, gsb[:, st, :], ALU.mult, ALU.mult, [ob, ('rsq', st), ('gs', s_, st)], [('og', st % 2)])

    def head_out_T(h8, st):
        pb = P.rot('pt', ['pt0', 'pt1'])
        for j in range(4):
            tr(PS[pb][:, j * 128:(j + 1) * 128], og2[st % 2][:, j * 128:(j + 1) * 128], [('og', st % 2)], [pb])
        tt(mixedT[:, h8 * 4:(h8 + 1) * 4, st * 128:(st + 1) * 128],
           PS[pb][:, :].rearrange("p (a b) -> p a b", a=4), gnorm_bc(h8 * 4), ALU.mult, [pb, 'cst'], [('mixT', h8)])

    def gla_fseq(h, s_):
        vb = v_bf2[s_]
        for st in range(4):
            sl = slice(st * 128, (st + 1) * 128)
            ob = 'mO'
            mm(PS[ob][:, :], ATa[:, st, :], vb[:, st, :], True, False, [('AT', st), ('v_bf', s_, st)], [ob])
            for dc in range(2):
                mm(PS[ob][:, :], qtla[:, dc, sl], S_bf[:, dc, :], False, dc == 1, [('qtl', st), 'S_bf'], [ob])
            P.mark()
            for dc in range(2):
                hp, hk = hb512()
                mm(hp, kha[:, st, dc * 128:(dc + 1) * 128], vb[:, st, :], True, True, [('kh', st), ('v_bf', s_, st)], [hk])
                stt(S_f[:, dc, :], S_f[:, dc, :], E1a[:, dc, st * 128 + 127:st * 128 + 128], hp,
                    ALU.mult, ALU.add, ['S_f', ('E1', st), hk], ['S_f'])
            cp(S_bf[:, :, :], S_f[:, :, :], ['S_f'], ['S_bf'], eng='act')
            P.mark()
            head_norm_out(h, s_, st, ob, None)
            P.mark()
            if st > 0:
                head_out_T(h, st - 1)
                P.mark()
        head_out_T(h, 3)
        dma('sp', st_gla[h].rearrange("p (a b) -> p a b", a=2), S_f[:, :, :], ['S_f'], [('st_gla', h)])

    def mlstm_fproj(h, s_, part):
        vb, gsb = v_bf2[s_], gs_b2[s_]
        km_f, qm_f = kT_f, qT_f
        if part == 'a1':
            def ev_v(pc, st, ps, pb):
                cp(vb[:, st, pc * 256:(pc + 1) * 256], ps, [pb], [('v_bf', s_, st)], eng='dve')
            ptm(w_in, 8208 + h * 512, 2, ev_v)
        elif part == 'a2':
            def ev_o(pc, st, ps, pb):
                act(gsb[:, st, pc * 256:(pc + 1) * 256], ps, AF.Sigmoid, [pb], [('gs', s_, st)])
            ptm(w_in, 10256 + h * 512, 2, ev_o)
        else:
            halo_in(xk, 8 + h * 2, 'xk')
            halo_in(xq, h * 2, 'xq')

            def ev_k(m, ps, pb):
                cp(xk[:, m, 3:515], ps, [pb], ['xk'], eng='act')
            pfm(w_in, 7184 + h * 256, 2, ev_k)

            def ev_q(m, ps, pb):
                cp(xq[:, m, 3:515], ps, [pb], ['xq'], eng='act')
            pfm(w_in, 6160 + h * 256, 2, ev_q)
            conv_silu(xk, 8 + h * 2, km_f, 'xk', 'kT_f')
            halo_out(xk, 8 + h * 2, 'xk')
            cp(k_bf[:, :, :], km_f[:, :, :], ['kT_f'], ['k_bf'], eng='dve')
            conv_silu(xq, h * 2, qm_f, 'xq', 'qT_f')
            cp(q_bf[:, :, :], qm_f[:, :, :], ['qT_f'], ['q_bf'], eng='dve')
            halo_out(xq, h * 2, 'xq')

    def mlstm_fpre(h, s_, ti):
        km_f, qm_f = kT_f, qT_f
        if ti == 0:
            ms(S_f[:, :, :], 0.0, ['S_f'])
            ms(n_f, 0.0, ['n_f'])
        else:
            dma('sp', S_f[:, :, :], st_mC[h].rearrange("p (a b) -> p a b", a=2), [('st_mC', h)], ['S_f'])
            dma('sp', n_f, st_mn[h], [('st_mn', h)], ['n_f'])
        cp(S_bf[:, :, :], S_f[:, :, :], ['S_f'], ['S_bf'], eng='act')
        cp(n_bf, n_f, ['n_f'], ['n_bf'], eng='dve')
        for st in range(4):
            sl = slice(st * 128, (st + 1) * 128)
            Lf = Lf2[st % 2]
            ts(Lf, onesf, spf[:, st, h:h + 1], None, ALU.mult, None, ['cst', ('spf', st)], [('Lf', st % 2)])
            hp, hk = hb()
            mm(hp[:, 0:128], Lf, Uneg, True, True, [('Lf', st % 2), 'cst'], [hk])
            act(EBa[:, st, :], hp[:, 0:128], AF.Exp, [hk], [('EB', st)])
            act(wexpa[:, st:st + 1], hp[:, 127:128], AF.Exp, [hk, ('colb', st)], [('wexp', st)], bias=colb[:, st, h:h + 1])
            act(DTa[:, st, :], hp[:, 0:128], AF.Exp, [hk, ('colbD', st)], [('DT', st)], bias=colbD[:, st, h:h + 1])
            tt(DTa[:, st, :], DTa[:, st, :], tri, ALU.mult, [('DT', st), 'cst'], [('DT', st)])
            for dc in range(2):
                stt(qtla[:, dc, sl], qm_f[:, dc, sl], 1.0 / 16.0, EBa[:, st, :], ALU.mult, ALU.mult,
                    ['qT_f', ('EB', st)], [('qtl', st)])
            pb = P.rot('pt', ['pt0', 'pt1'])
            for dc in range(2):
                tr(PS[pb][:, dc * 128:(dc + 1) * 128], km_f[:, dc, sl], ['kT_f'], [pb])
            ts(kha[:, st, :], PS[pb][:, 0:256], wexpa[:, st:st + 1], None, ALU.mult, None, [pb, ('wexp', st)], [('kh', st)])
            P.mark()
        for st in range(4):
            sl = slice(st * 128, (st + 1) * 128)
            hp, hk = hb()
            for dc in range(2):
                mm(hp[:, 0:128], k_bf[:, dc, sl], q_bf[:, dc, sl], dc == 0, dc == 1, ['k_bf', 'q_bf'], [hk])
            tt(ATa[:, st, :], hp[:, 0:128], DTa[:, st, :], ALU.mult, [hk, ('DT', st)], [('AT', st)])
            P.mark()

    def mlstm_fseq(h, s_):
        vb = v_bf2[s_]
        for st in range(4):
            sl = slice(st * 128, (st + 1) * 128)
            ob = 'mO'
            mm(PS[ob][:, :], ATa[:, st, :], vb[:, st, :], True, False, [('AT', st), ('v_bf', s_, st)], [ob])
            for dc in range(2):
                mm(PS[ob][:, :], qtla[:, dc, sl], S_bf[:, dc, :], False, dc == 1, [('qtl', st), 'S_bf'], [ob])
            dp, dk_ = hb()
            mm(dp[:, 0:1], ATa[:, st, :], ONESB[:, 0:1], True, False, [('AT', st), 'cst'], [dk_])
            for dc in range(2):
                mm(dp[:, 0:1], qtla[:, dc, sl], n_bf[:, dc:dc + 1], False, dc == 1, [('qtl', st), 'n_bf'], [dk_])
            dn = sm2[:, st * 4 + 2:st * 4 + 3]
            act(dn, dp[:, 0:1], AF.Abs, [dk_], [('dn', st)])
            ts(dn, dn, 1.0, None, ALU.max, None, [('dn', st)], [('dn', st)])
            P.op('dve', lambda e, a=dn: e.reciprocal(out=a, in_=a), [('dn', st)], [('dn', st)])
            P.mark()
            np_, nk = hb()
            for dc in range(2):
                hp, hk = hb512()
                mm(hp, kha[:, st, dc * 128:(dc + 1) * 128], vb[:, st, :], True, True, [('kh', st), ('v_bf', s_, st)], [hk])
                stt(S_f[:, dc, :], S_f[:, dc, :], EBa[:, st, 127:128], hp, ALU.mult, ALU.add,
                    ['S_f', ('EB', st), hk], ['S_f'])
                mm(np_[:, dc:dc + 1], kha[:, st, dc * 128:(dc + 1) * 128], ONESB[:, 0:1], True, True,
                   [('kh', st), 'cst'], [nk])
            stt(n_f, n_f, EBa[:, st, 127:128], np_[:, 0:2], ALU.mult, ALU.add, ['n_f', ('EB', st), nk], ['n_f'])
            cp(S_bf[:, :, :], S_f[:, :, :], ['S_f'], ['S_bf'], eng='act')
            cp(n_bf, n_f, ['n_f'], ['n_bf'], eng='dve')
            P.mark()
            head_norm_out(4 + h, s_, st, ob, dn)
            P.mark()
            if st > 0:
                head_out_T(4 + h, st - 1)
                P.mark()
        head_out_T(4 + h, 3)
        dma('sp', st_mC[h].rearrange("p (a b) -> p a b", a=2), S_f[:, :, :], ['S_f'], [('st_mC', h)])
        dma('sp', st_mn[h], n_f, ['n_f'], [('st_mn', h)])

    def full_mixers(ti):
        def proj(i, part):
            P.begin()
            if i < 4:
                gla_fproj(i, i % 2, part)
            else:
                mlstm_fproj(i - 4, i % 2, part)
            return P.end()

        def pre(i):
            P.begin()
            if i < 4:
                gla_fpre(i, i % 2, ti)
            else:
                mlstm_fpre(i - 4, i % 2, ti)
            return P.end()

        def seq(i):
            P.begin()
            if i < 4:
                gla_fseq(i, i % 2)
            else:
                mlstm_fseq(i - 4, i % 2)
            return P.end()
        for part in ('a1', 'a2', 'b'):
            P.commit(proj(0, part))
        for i in range(8):
            if i < 7:
                P.commit(Prog.merge(pre(i), proj(i + 1, 'a1')))
                P.commit(Prog.merge(seq(i), proj(i + 1, 'a2') + [None] + proj(i + 1, 'b')))
            else:
                P.commit(pre(i))
                P.commit(seq(i))

    def addB(st, c0, n, ps, pb):
        keys = [('B', st, c) for c in range(c0 // 256, (c0 + n) // 256)]
        tt(Bx[:, st, c0:c0 + n], Bx[:, st, c0:c0 + n], ps, ALU.add, keys + [pb], keys)

    def load_x(ti):
        for st in range(4):
            dma('sp', Bx[:, st, :], xs[ti * T + st * 128:ti * T + (st + 1) * 128, :], [], bkeys(st))

    def tile_full(ti, oi):
        fenceC()
        load_x(ti)
        rmsnorm_T(4, lambda st: Bx[:, st, :], lambda st: bkeys(st), C_GMIX)
        fenceB()
        fenceC()
        gates_tile(True)
        full_mixers(ti)
        fenceB()
        load_x(ti)
        for cb in range(16):
            slot, skey = load_w(w_out[:, cb * 256:(cb + 1) * 256].rearrange("(k p) n -> p k n", p=128), 32, 256)
            for st in range(4):
                pb = P.rot('pj', ['pj0', 'pj1'])
                for kc in range(32):
                    mm(PS[pb][:, 0:256], mixedT[:, kc, st * 128:(st + 1) * 128], slot[:, kc, :], kc == 0, kc == 31,
                       [skey, ('mixT', kc // 4)], [pb])
                addB(st, cb * 256, 256, PS[pb][:, 0:256], pb)
        if stop != 'mix':
          tile_cross()
        if stop not in ('mix', 'cross'):
          tile_ffn()
        tile_final(oi)

    def tile_cross():
        fenceC()
        rmsnorm_T(4, lambda st: Bx[:, st, :], lambda st: bkeys(st), C_GCR)
        fenceC()
        qc = Cc[:, 0:4096].rearrange("p (a b) -> p a b", a=8)
        ocT = Cc[:, 4096:8192].rearrange("p (a b) -> p a b", a=8)
        KTh = Cc[:, 8192:10240].rearrange("p (a b) -> p a b", a=8)
        Vh = Cc[:, 10240:12288].rearrange("p (a b) -> p a b", a=2)
        eT = Cc[:, 12288:13312].rearrange("p (a b) -> p a b", a=2)
        rden = Cc[:, 13312:14336].bitcast(F32)
        for h in range(4):
            def ev_q(m, ps, pb):
                cp(qc[:, m, :], ps, [pb], ['qc'], eng='act' if m % 2 else 'dve')
            proj_fm(wq, h * 1024, 8, ev_q)
            dma('sp', KTh, KT_d[h * 8:(h + 1) * 8].rearrange("k p m -> p k m"), ['KT_d'], ['KTh'])
            for mc in range(2):
                dma('sp', Vh[:, mc, :], V_d[mc][:, h * 1024:(h + 1) * 1024], ['V_d'], ['Vh'])
            for mc in range(2):
                pb = ['mS', 'mO'][mc]
                for dc in range(8):
                    mm(PS[pb][:, :], KTh[:, dc, mc * 128:(mc + 1) * 128], qc[:, dc, :], dc == 0, dc == 7,
                       ['KTh', 'qc'], [pb])
                act(eT[:, mc, :], PS[pb][:, :], AF.Exp, [pb], ['eT'], scale=1.0 / 32.0)
            for mc in range(2):
                mm(PS['mA'][:, :], ONESB[:, :], eT[:, mc, :], mc == 0, mc == 1, ['cst', 'eT'], ['mA'])
            P.op('dve', lambda e, a=rden: e.reciprocal(out=a, in_=PS['mA'][:, :]), ['mA'], ['rden'])
            for dc in range(8):
                pb = P.rot('pj', ['pj0', 'pj1'])
                for mc in range(2):
                    mm(PS[pb][:, :], Vh[:, mc, dc * 128:(dc + 1) * 128], eT[:, mc, :], mc == 0, mc == 1,
                       ['Vh', 'eT'], [pb])
                tt(ocT[:, dc, :], PS[pb][:, :], rden, ALU.mult, [pb, 'rden'], ['ocT'])
            for cb in range(4):
                slot, skey = load_w(wo[h * 1024:(h + 1) * 1024, cb * 1024:(cb + 1) * 1024].rearrange("(k p) n -> p k n", p=128), 8, 1024)
                for st in range(4):
                    for hf in range(2):
                        pb = P.rot('pj', ['pj0', 'pj1'])
                        for kc in range(8):
                            mm(PS[pb][:, :], ocT[:, kc, st * 128:(st + 1) * 128], slot[:, kc, hf * 512:(hf + 1) * 512],
                               kc == 0, kc == 7, [skey, 'ocT'], [pb])
                        addB(st, cb * 1024 + hf * 512, 512, PS[pb][:, :], pb)
    def tile_ffn():
        fenceC()
        rmsnorm_T(4, lambda st: Bx[:, st, :], lambda st: bkeys(st), C_GFFN)
        fenceC()
        hidT = Cc[:, 0:11264].rearrange("p (a b) -> p a b", a=22)
        sg = [Cc[:, 11264 + i * 1024:11264 + (i + 1) * 1024].bitcast(F32) for i in range(2)]
        for (hc0, nch) in HID_BLOCKS:
            ci = 0
            while ci < nch:
                nm = min(2, nch - ci)
                col = (hc0 + ci) * 128
                sg_, kg = load_w(w_gate[:, col:col + nm * 128].rearrange("(k p) n -> p k n", p=128), 32, nm * 128)
                su_, ku = load_w(w_up[:, col:col + nm * 128].rearrange("(k p) n -> p k n", p=128), 32, nm * 128)
                for mi in range(nm):
                    i = (ci + mi) % 2
                    bg, bu = [('mS', 'mO'), ('mA', 'mU')][i]
                    for kc in range(32):
                        mm(PS[bg][:, :], sg_[:, kc, mi * 128:(mi + 1) * 128], nT[:, kc, :], kc == 0, kc == 31,
                           [kg] + NTK, [bg])
                    for kc in range(32):
                        mm(PS[bu][:, :], su_[:, kc, mi * 128:(mi + 1) * 128], nT[:, kc, :], kc == 0, kc == 31,
                           [ku] + NTK, [bu])
                    act(sg[i], PS[bg][:, :], AF.Silu, [bg], [('sg', i)])
                    tt(hidT[:, ci + mi, :], sg[i], PS[bu][:, :], ALU.mult, [('sg', i), bu], ['hidT'])
                ci += nm
            for cb in range(16):
                slot, skey = load_w(w_down[hc0 * 128:(hc0 + nch) * 128, cb * 256:(cb + 1) * 256].rearrange("(k p) n -> p k n", p=128), nch, 256)
                for st in range(4):
                    pb = P.rot('pj', ['pj0', 'pj1'])
                    for kc in range(nch):
                        mm(PS[pb][:, 0:256], hidT[:, kc, st * 128:(st + 1) * 128], slot[:, kc, :], kc == 0, kc == nch - 1,
                           [skey, 'hidT'], [pb])
                    addB(st, cb * 256, 256, PS[pb][:, 0:256], pb)
    def tile_final(oi):
        fenceC()
        dma('sp', gfin_sb, gfin_d[:, :], [], ['gfin'])
        for st in range(4):
            ssv = SM[:, 2 * st:2 * st + 1]
            rsv = SM[:, 2 * st + 1:2 * st + 2]
            ms(ssv, 0.0, [('ss', st)])
            act(junk, Bx[:, st, :], AF.Square, bkeys(st), [('ss', st)], accum=ssv)
            ts(rsv, ssv, 1.0 / D, EPS, ALU.mult, ALU.add, [('ss', st)], [('rs', st)])
            act(rsv, rsv, AF.Ln, [('rs', st)], [('rs', st)])
            act(rsv, rsv, AF.Exp, [('rs', st)], [('rs', st)], scale=-0.5)
            stt(Bx[:, st, :], Bx[:, st, :], rsv, gfin_sb, ALU.mult, ALU.mult, bkeys(st) + [('rs', st), 'gfin'], bkeys(st))
            dma('sp', out[oi * T + st * 128:oi * T + (st + 1) * 128, :], Bx[:, st, :], bkeys(st), [('out', oi, st)])

    def tile_state(ti, want_q_halo):
        fenceC()
        load_x(ti)
        rmsnorm_T(4, lambda st: Bx[:, st, :], lambda st: bkeys(st), C_GMIX)
        fenceB()
        gates_tile(False)
        state_mixers(ti, want_q_halo)
        fenceB()

    TEMP_KEYS = (['kT_f', 'qT_f', 'xk', 'xq', 'S_f', 'S_bf', 'n_f', 'n_bf', 'ef', 'nflag', 'k_bf', 'q_bf']
                 + [(n, s_) for n in ('v_bf', 'gs', 'sp', 'E1', 'E2', 'ktl', 'qtl', 'kh', 'AT', 'ssq', 'rsq', 'dn', 'gi',
                                     'spf', 'colb', 'colbD', 'wexp', 'EB', 'DT') for s_ in range(4)]
                 + [(n, s_) for n in ('khT', 'og', 'elog', 'Lf') for s_ in range(2)]
                 + [('skT', 0), ('skT', 1), ('sxk', 0), ('sxk', 1), 'xq_st', 'og', ('og', 0), ('og', 1)]
                 + [(n, a_, b_) for n in ('v_bf', 'gs') for a_ in range(2) for b_ in range(4)]
                 + [(n, a_, b_) for n in ('sv', 'ssp') for a_ in range(2) for b_ in range(4)])
    ALLB = [k for st in range(4) for k in bkeys(st)]

    CKEYS = [('mixT', i) for i in range(8)] + ['qc', 'ocT', 'KTh', 'Vh', 'eT', 'rden', 'hidT', ('sg', 0), ('sg', 1),
             'gfin', ('xn', 0), ('xn', 1), ('ktb', 0), ('ktb', 1), ('vtb', 0), ('vtb', 1)]

    def fenceB():
        P.op('dve', lambda e: e.memset(SM[:, 100:101], 0.0), [], ALLB + TEMP_KEYS)

    def fenceC():
        P.op('dve', lambda e: e.memset(SM[:, 101:102], 0.0), [], CKEYS)

    phase_mem()
    for ti in range(NT):
        if ti < npre:
            tile_state(ti, last_pre_q and ti == npre - 1)
        else:
            tile_full(ti, ti - npre)
    P.emit()
    nc._prog_stats = P.stats
    return nc


def _pack_consts(inp, flags):
    c = np.zeros((128, NCST), np.float32)
    c[:, C_ID:C_ID + 128] = np.eye(128, dtype=np.float32)
    tri = np.triu(np.ones((128, 128), np.float32))
    c[:, C_UG:C_UG + 128] = tri * (-1.0 / 16.0)
    c[:, C_UN:C_UN + 128] = -tri
    c[:, C_TRI:C_TRI + 128] = tri
    c[:, C_ONE:C_ONE + 128] = 1.0

    def fm(v):
        return np.ascontiguousarray(np.asarray(v, np.float32).reshape(32, 128).T)
    c[:, C_GMIX:C_GMIX + 32] = fm(inp["norm_mix_g"][0])
    c[:, C_GCR:C_GCR + 32] = fm(inp["norm_cross_g"][0])
    c[:, C_GFFN:C_GFFN + 32] = fm(inp["norm_ffn_g"][0])
    c[:, C_GMEM:C_GMEM + 32] = fm(inp["norm_mem_g"][0])
    gn = np.concatenate([np.asarray(inp["gla_norm_g"][0]).reshape(-1), np.asarray(inp["mlstm_norm_g"][0]).reshape(-1)])
    c[:, C_GNORM:C_GNORM + 32] = fm(gn)
    cw = np.asarray(inp["mlstm_conv_w"][0], np.float32)
    c[:, C_CW:C_CW + 64] = cw.T.reshape(16, 128, 4).transpose(1, 0, 2).reshape(128, 64)
    c[:, C_CB:C_CB + 16] = np.asarray(inp["mlstm_conv_b"][0], np.float32).reshape(16, 128).T
    c[:, C_BIF:C_BIF + 4] = np.asarray(inp["mlstm_igate_b"][0], np.float32)[None, :]
    c[:, C_BIF + 4:C_BIF + 8] = np.asarray(inp["mlstm_fgate_b"][0], np.float32)[None, :]
    c[:, C_FLAG:C_FLAG + len(flags)] = np.asarray(flags, np.float32)[None, :]
    return c


def _shared_maps(inp):
    w2 = np.zeros((32, 1024), np.float32)
    w2[0:16] = np.asarray(inp["gla_gate_w2"][0], np.float32)
    w2[16] = np.asarray(inp["gla_gate_b"][0], np.float32)
    m = {
        "w_in": np.ascontiguousarray(inp["w_in"][0]), "w_out": np.ascontiguousarray(inp["w_out"][0]),
        "wq": np.ascontiguousarray(inp["wq_c"][0]), "wk": np.ascontiguousarray(inp["wk_c"][0]),
        "wv": np.ascontiguousarray(inp["wv_c"][0]), "wo": np.ascontiguousarray(inp["wo_c"][0]),
        "w_gate": np.ascontiguousarray(inp["w_gate"][0]), "w_up": np.ascontiguousarray(inp["w_up"][0]),
        "w_down": np.ascontiguousarray(inp["w_down"][0]),
        "w2aug": w2,
        "gfin": np.ascontiguousarray(np.broadcast_to(np.asarray(inp["norm_final_g"], np.float32)[None, :], (128, D))),
    }
    return m


NPRE_FULL = 12
NOWN_FULL = 4


def kernel(**inputs):
    inp = {k: np.asarray(v) for k, v in inputs.items()}
    x = inp["x"]
    mem = inp["mem"]
    nc = build(NPRE_FULL, NOWN_FULL)
    shared = _shared_maps(inp)
    in_maps = []
    for c in range(8):
        b, j = c // 4, c % 4
        nz = (NPRE_FULL - 4 * j) * T
        xs = np.concatenate([np.zeros((nz, D), np.float32), x[b, 0:(4 * j + 4) * T]], axis=0)
        flags = [1.0 if i >= NPRE_FULL - 4 * j else 0.0 for i in range(NPRE_FULL)]
        m = dict(shared)
        m["xs"] = np.ascontiguousarray(xs)
        m["mem"] = np.ascontiguousarray(mem[b])
        m["cst"] = _pack_consts(inp, flags)
        in_maps.append(m)
    res = run_bass_kernel_spmd(nc, in_maps, core_ids=list(range(8)))
    outp = np.zeros((2, 8192, D), np.float32)
    for c in range(8):
        b, j = c // 4, c % 4
        outp[b, j * 2048:(j + 1) * 2048] = res.results[c]["out"]
    return outp
```
